# Optimizing a Trainium2 kernel written in Bass

```python
import jax, jax.numpy as jnp
from jax import lax
import numpy as np

D_MODEL = 2048
BATCH = 2
SEQ = 4096
DEPTH = 2

HEAD_DIM = 64
A_HEADS = 8
A_WIDTH = A_HEADS * HEAD_DIM
CHUNK = 128
B_WIDTH = 768
CONV_WIDTH = 3
DILATION_PATTERNS = ((128, 1), (512, 4), (2048, 16))
C_HEADS_PER_PATTERN = 4
C_HEADS = C_HEADS_PER_PATTERN * len(DILATION_PATTERNS)
C_WIDTH = C_HEADS * HEAD_DIM
D_MIX = A_WIDTH + B_WIDTH + C_WIDTH
PROJ_SIZES = (A_WIDTH, A_WIDTH, B_WIDTH, B_WIDTH, B_WIDTH, C_WIDTH, C_WIDTH, C_WIDTH)
PROJ_SPLITS = tuple(int(s) for s in np.cumsum(PROJ_SIZES)[:-1])
D_IN_PROJ = sum(PROJ_SIZES)
D_FF = 4 * D_MODEL
EPS = 1e-6

kernel_name = 'hymba_style_sgu_shortconv_dilated_attn'


def rms_norm(x, g):
    x32 = x.astype(jnp.float32)
    y = x32 * lax.rsqrt(jnp.mean(x32 * x32, axis=-1, keepdims=True) + EPS)
    return (y * g.astype(jnp.float32)).astype(x.dtype)


def spatial_gating(u, v, w_s, b_s):
    bsz, s = u.shape[:2]
    n_chunks = s // CHUNK
    vc = v.reshape(bsz, n_chunks, CHUNK, A_HEADS, HEAD_DIM)
    w_causal = jnp.tril(w_s)
    mixed = jnp.einsum('hij,bcjhd->bcihd', w_causal, vc) + b_s.T[None, None, :, :, None]
    return u * mixed.reshape(bsz, s, A_WIDTH)


def short_gated_conv(b_gate, c_gate, xb, w_conv):
    z = c_gate * xb
    zp = jnp.pad(z, ((0, 0), (CONV_WIDTH - 1, 0), (0, 0)))
    s = z.shape[1]
    conv = sum(w_conv[i] * zp[:, i:i + s] for i in range(CONV_WIDTH))
    return b_gate * conv


def head_rms_norm(x, g):
    x32 = x.astype(jnp.float32)
    return x32 * lax.rsqrt(jnp.mean(x32 * x32, axis=-1, keepdims=True) + EPS) * g.astype(jnp.float32)


def dilated_window_attention(q, k, v, window, dilation):
    bsz, s, h, d = q.shape
    length = s // dilation
    blk = window // dilation
    nb = -(-length // blk)
    lp = nb * blk

    def to_sub(t):
        t = t.reshape(bsz, length, dilation, h, d).transpose(0, 2, 1, 3, 4)
        t = jnp.pad(t, ((0, 0), (0, 0), (0, lp - length), (0, 0), (0, 0)))
        return t.reshape(bsz, dilation, nb, blk, h, d)

    qb = to_sub(q)
    kb = to_sub(k)
    vb = to_sub(v.astype(jnp.float32))
    pad_prev = ((0, 0), (0, 0), (1, 0), (0, 0), (0, 0), (0, 0))
    kcat = jnp.concatenate([jnp.pad(kb, pad_prev)[:, :, :-1], kb], axis=3)
    vcat = jnp.concatenate([jnp.pad(vb, pad_prev)[:, :, :-1], vb], axis=3)

    scores = jnp.einsum('brnqhd,brnkhd->brnhqk', qb, kcat) * (HEAD_DIM ** -0.5)
    qi = jnp.arange(blk)[:, None]
    kj = jnp.arange(2 * blk)[None, :]
    band = (kj >= qi) & (kj <= qi + blk)
    has_prev = jnp.arange(nb)[:, None, None] > 0
    mask = band[None] & (has_prev | (kj >= blk)[None])
    scores = jnp.where(mask[None, None, :, None], scores, -jnp.inf)
    m = jnp.max(scores, axis=-1, keepdims=True)
    e = jnp.exp(scores - m)
    den = jnp.sum(e, axis=-1, keepdims=True)
    o = jnp.einsum('brnhqk,brnkhd->brnhqd', e, vcat) / den
    lse = (m + jnp.log(den))[..., 0]

    o = o.transpose(0, 1, 2, 4, 3, 5).reshape(bsz, dilation, lp, h, d)[:, :, :length]
    o = o.transpose(0, 2, 1, 3, 4).reshape(bsz, s, h, d)
    lse = lse.transpose(0, 1, 2, 4, 3).reshape(bsz, dilation, lp, h)[:, :, :length]
    lse = lse.transpose(0, 2, 1, 3).reshape(bsz, s, h)
    return o, lse


def dilated_mixture(q, k, v, q_g, k_g):
    bsz, s = q.shape[:2]
    q = head_rms_norm(q.reshape(bsz, s, C_HEADS, HEAD_DIM), q_g)
    k = head_rms_norm(k.reshape(bsz, s, C_HEADS, HEAD_DIM), k_g)
    v = v.reshape(bsz, s, C_HEADS, HEAD_DIM)
    outs, lses = [], []
    for g, (window, dilation) in enumerate(DILATION_PATTERNS):
        sl = slice(g * C_HEADS_PER_PATTERN, (g + 1) * C_HEADS_PER_PATTERN)
        o, lse = dilated_window_attention(q[:, :, sl], k[:, :, sl], v[:, :, sl], window, dilation)
        outs.append(o)
        lses.append(lse)
    alpha = jax.nn.softmax(jnp.stack(lses, axis=0), axis=0)
    y = jnp.stack(outs, axis=0) * alpha[..., None]
    return y.transpose(1, 2, 0, 3, 4).reshape(bsz, s, C_WIDTH).astype(v.dtype)


def setup_inputs(seed: int = 0) -> dict:
    key = jax.random.key(seed)
    ks = jax.random.split(key, 12)
    n = jax.random.normal
    f32 = jnp.float32
    return {
        'x': n(ks[0], (BATCH, SEQ, D_MODEL), f32),
        'attn_norm': 1.0 + 0.02 * n(ks[1], (DEPTH, D_MODEL), f32),
        'w_in': n(ks[2], (DEPTH, D_MODEL, D_IN_PROJ), f32) * D_MODEL ** -0.5,
        'sgu_w': n(ks[3], (DEPTH, A_HEADS, CHUNK, CHUNK), f32) * CHUNK ** -0.5,
        'sgu_b': 1.0 + 0.1 * n(ks[4], (DEPTH, A_HEADS, CHUNK), f32),
        'conv_w': n(ks[5], (DEPTH, CONV_WIDTH, B_WIDTH), f32) * CONV_WIDTH ** -0.5,
        'q_norm': 1.0 + 0.02 * n(ks[6], (DEPTH, HEAD_DIM), f32),
        'k_norm': 1.0 + 0.02 * n(ks[7], (DEPTH, HEAD_DIM), f32),
        'w_out': n(ks[8], (DEPTH, D_MIX, D_MODEL), f32) * D_MIX ** -0.5,
        'mlp_norm': 1.0 + 0.02 * n(ks[9], (DEPTH, D_MODEL), f32),
        'w_mlp_in': n(ks[10], (DEPTH, D_MODEL, D_FF), f32) * D_MODEL ** -0.5,
        'w_mlp_out': n(ks[11], (DEPTH, D_FF, D_MODEL), f32) * D_FF ** -0.5,
    }


def reference(x, attn_norm, w_in, sgu_w, sgu_b, conv_w, q_norm, k_norm, w_out,
              mlp_norm, w_mlp_in, w_mlp_out):
    for l in range(DEPTH):
        h = rms_norm(x, attn_norm[l])
        p = h @ w_in[l]
        a_u, a_v, b_b, b_c, b_x, q, k, v = jnp.split(p, PROJ_SPLITS, axis=-1)
        y_a = spatial_gating(a_u, a_v, sgu_w[l], sgu_b[l])
        y_b = short_gated_conv(b_b, b_c, b_x, conv_w[l])
        y_c = dilated_mixture(q, k, v, q_norm[l], k_norm[l])
        x = x + jnp.concatenate([y_a, y_b, y_c], axis=-1) @ w_out[l]
        h = rms_norm(x, mlp_norm[l])
        x = x + jnp.square(jax.nn.relu(h @ w_mlp_in[l])) @ w_mlp_out[l]
    return x
```

```python
import contextlib
import numpy as np
import ml_dtypes
import concourse.bass as bass
import concourse.mybir as mybir
from concourse.bass_utils import run_bass_kernel_spmd

F32 = mybir.dt.float32
BF16 = mybir.dt.bfloat16
I32 = mybir.dt.int32
AF = mybir.ActivationFunctionType
ALU = mybir.AluOpType
AX = mybir.AxisListType
NPBF = ml_dtypes.bfloat16

ENGS = ("pe", "act", "dve", "pool", "sp")
T = 1024
D = 2048
DIN = 5632
DFF = 8192
EPS = 1e-6
PATS = ((1, 128), (4, 512), (16, 2048))


class Op:
    __slots__ = ("eng", "fn", "deps", "dma", "key", "ms", "has_cons", "gidx", "inc", "nosync")

    def __init__(self, eng, fn, dma, key, gidx, inc):
        self.eng = eng
        self.fn = fn
        self.deps = []
        self.dma = dma
        self.key = key
        self.ms = None
        self.has_cons = False
        self.gidx = gidx
        self.inc = inc
        self.nosync = False


class Prog:
    def __init__(self, nc, same_engine_sync=True):
        self.nc = nc
        self.q = {e: [] for e in ENGS}
        self.last_w = {}
        self.readers = {}
        self.n = 0
        self.same_engine_sync = same_engine_sync

    def add(self, eng, fn, reads=(), writes=(), dma=False, key=None, inc=None, grp=False, nosync=False):
        if dma and key is None:
            key = writes[0]
        op = Op(eng, fn, dma, key, self.n, inc if inc is not None else (16 if dma else 1))
        op.nosync = nosync
        self.n += 1
        deps = {}
        for r in reads:
            w = self.last_w.get(r)
            if w is not None:
                deps[id(w)] = w
        for w_ in writes:
            w = self.last_w.get(w_)
            if w is not None and not (grp and w.dma and w.key == key):
                deps[id(w)] = w
            for rd in self.readers.get(w_, ()):
                deps[id(rd)] = rd
        op.deps = list(deps.values())
        for d in op.deps:
            d.has_cons = True
        for r in reads:
            self.readers.setdefault(r, []).append(op)
        for w_ in writes:
            self.last_w[w_] = op
            self.readers[w_] = []
        self.q[eng].append(op)
        return op

    def emit(self, final_waits=()):
        nc = self.nc
        eng_cnt = {e: 0 for e in ENGS}
        key_cnt = {}
        allops = sorted([o for e in ENGS for o in self.q[e]], key=lambda o: o.gidx)
        for o in final_waits:
            o.has_cons = True
        for o in allops:
            if o.dma:
                key_cnt[o.key] = key_cnt.get(o.key, 0) + o.inc
                o.ms = key_cnt[o.key]
            elif o.has_cons and not o.nosync:
                eng_cnt[o.eng] += 1
                o.ms = eng_cnt[o.eng]
        keys = sorted(key_cnt.keys(), key=str)
        with contextlib.ExitStack() as st:
            esem = {e: st.enter_context(nc.semaphore("s_" + e)) for e in ENGS}
            ksem = {k: st.enter_context(nc.semaphore("k%d" % i)) for i, k in enumerate(keys)}
            block = st.enter_context(nc.Block())
            self.nsem = len(esem) + len(ksem)

            def run(ename, h):
                waited = {}
                for o in self.q[ename]:
                    for d in o.deps:
                        if d.nosync:
                            continue
                        if d.dma:
                            s, v = ksem[d.key], d.ms
                        else:
                            if d.eng == ename and (ename == "pe" or not self.same_engine_sync):
                                continue
                            s, v = esem[d.eng], d.ms
                        if waited.get(id(s), 0) >= v:
                            continue
                        waited[id(s)] = v
                        h.wait_ge(s, v)
                    ins = o.fn(h)
                    if o.dma:
                        ins.then_inc(ksem[o.key], o.inc)
                    elif o.has_cons and not o.nosync:
                        ins.then_inc(esem[o.eng], 1)
                if ename == "sp":
                    for d in final_waits:
                        if d.nosync:
                            continue
                        s, v = (ksem[d.key], d.ms) if d.dma else (esem[d.eng], d.ms)
                        h.wait_ge(s, v)

            @block.tensor
            def _(h):
                run("pe", h)

            @block.scalar
            def _(h):
                run("act", h)

            @block.vector
            def _(h):
                run("dve", h)

            @block.gpsimd
            def _(h):
                run("pool", h)

            @block.sync
            def _(h):
                run("sp", h)


class Builder:
    def __init__(self, mode):
        self.mode = mode
        nc = self.nc = bass.Bass("TRN2", target_bir_lowering=False)
        self.st = contextlib.ExitStack()
        self.P = Prog(nc)
        self.bank = 0
        self.wcnt = 0
        self.wissued = 0
        self.wplan = []
        self.finals = []
        self.evq = 0
        self.layers = {"A": [0], "B": [0, 1], "C": [1], "F": [0, 1], "G": [0]}[mode]
        self._declare()

    def dram_in(self, name, shape, dt):
        return self.nc.dram_tensor(name, list(shape), dt, kind="ExternalInput").ap()

    def dram_out(self, name, shape, dt):
        return self.nc.dram_tensor(name, list(shape), dt, kind="ExternalOutput").ap()

    def dram_int(self, name, shape, dt):
        return self.nc.dram_tensor(name, list(shape), dt).ap()

    def sb(self, name, shape, dt):
        return self.st.enter_context(self.nc.sbuf_tensor(name, list(shape), dt))

    def _declare(self):
        m = self.mode
        if m in ("A", "B", "F", "G"):
            self.x_in = self.dram_in("x", [T, D], F32)
        if m == "C":
            self.xt_in = self.dram_in("xt_in", [128, 16, T], F32)
        if m == "B":
            self.xt_out = self.dram_out("xt_out", [128, 16, T], F32)
        if m in ("C", "F"):
            self.y_out = self.dram_out("y", [T, D], F32)
        self.w_in = {}
        self.w_out = {}
        self.w1 = {}
        self.w2 = {}
        for l in self.layers:
            self.w_in[l] = self.dram_in("w_in%d" % l, [D, DIN], F32)
            full = not (m in ("A", "G") or (m == "B" and l == 1))
            if full:
                self.w_out[l] = self.dram_in("w_out%d" % l, [D, D], F32)
                self.w1[l] = self.dram_in("w1_%d" % l, [D, DFF], F32)
                self.w2[l] = self.dram_in("w2_%d" % l, [DFF, D], F32)
        self.gn_in = self.dram_in("gn", [128, 2, 2, 16], F32)
        self.cw_in = self.dram_in("cw", [128, 2, 6, 3], F32)
        self.qkg_in = self.dram_in("qkg", [128, 2, 2], F32)
        self.qkrow_in = self.dram_in("qkrow", [1, 2, 128], F32)
        self.sgwt_in = self.dram_in("sgwt", [128, 2, 8, 128], F32)
        self.sgb_in = self.dram_in("sgb", [2, 2, 4, 128], F32)
        self.cf_in = self.dram_in("cf", [128, 256], F32)
        self.cb_in = self.dram_in("cb", [128, 768], BF16)
        self.sel2_in = self.dram_in("sel2", [2, 128], F32)
        self.kx = {}
        self.vx = {}
        self.zx = {}
        self.kwin = {}
        self.vwin = {}
        self.zwin = {}
        for l in self.layers:
            export = (m == "A" and l == 0) or (m == "B" and l == 1)
            mk = self.dram_out if export else self.dram_int
            if m not in ("F", "G"):
                self.kx[l] = mk("kx%d" % l, [768, T], BF16)
                self.vx[l] = mk("vx%d" % l, [T, 768], BF16)
                self.zx[l] = mk("zx%d" % l, [768, 2], F32)
            post = (m == "B" and l == 0) or (m == "C" and l == 1)
            if post:
                self.kwin[l] = self.dram_in("kwin%d" % l, [3, 768, T], BF16)
                self.vwin[l] = self.dram_in("vwin%d" % l, [3 * T, 768], BF16)
                self.zwin[l] = self.dram_in("zwin%d" % l, [768, 2], F32)
        if m in ("F", "G"):
            for l in self.layers:
                self.kx[l] = self.dram_int("kx%d" % l, [768, T], BF16)
                self.vx[l] = self.dram_int("vx%d" % l, [T, 768], BF16)
                self.zx[l] = self.dram_int("zx%d" % l, [768, 2], F32)
                self.kwin[l] = [self.dram_int("kg%d_%d" % (l, h_), [6, 384, T], BF16) for h_ in range(2)]
                self.vwin[l] = [self.dram_int("vg%d_%d" % (l, h_), [6, 512, 768], BF16) for h_ in range(2)]
                self.zwin[l] = self.dram_int("zg%d" % l, [6, 768, 2], F32)
            self.kwl = {l: self.dram_int("kwl%d" % l, [3, 768, T], BF16) for l in self.layers}
            self.vwl = {l: self.dram_int("vwl%d" % l, [3 * T, 768], BF16) for l in self.layers}
            self.zwl = {l: self.dram_int("zwl%d" % l, [3, 768, 2], F32) for l in self.layers}
        self.XT = self.sb("XT", [128, 16, T], F32)
        self.HT = self.sb("HT", [128, 16, T], BF16)
        self.WS = [self.sb("WS%d" % i, [128, 8192], BF16) for i in range(3)]
        self.YB = [self.sb("YB%d" % i, [128, 6, T], BF16) for i in range(2)]
        self.S = self.sb("S", [128, 4, 1032], F32)
        self.BS = self.sb("BS", [128, 6, T], BF16)
        self.Z = self.S[:, 2, 0:T + 2]
        self.RS = self.S[:, 3, 0:T]
        self.SQ = self.BS[:, 4, :]
        self.KST = [self.BS[:, 0, :], self.BS[:, 1, :]]
        self.ET = [self.BS[:, 5, i * 256:(i + 1) * 256] for i in range(4)]
        self.VA = self.BS[:, 2:4, :].rearrange("p a t -> p (a t)").rearrange("p (n f) -> p n f", f=256)
        self.RT = [self.S[:, 0, 0:512], self.S[:, 1, 0:512]]
        self.GN = self.sb("GN", [128, 2, 2, 16], F32)
        self.CW = self.sb("CW", [128, 2, 6, 3], F32)
        self.QKG = self.sb("QKG", [128, 2, 2], F32)
        self.GQ8 = self.sb("GQ8", [128, 2], F32)
        self.QKROW = self.S[0:1, 3, 0:128]
        self.SGWT = self.S[:, 0, 0:T].rearrange("p (h i) -> p h i", h=8)
        self.WCT = self.sb("WCT", [128, 8, 128], BF16)
        self.SGB = self.S[0:2, 1, 0:512].rearrange("p (c i) -> p c i", c=4)
        self.CF = self.sb("CF", [128, 256], F32)
        self.CB = self.sb("CB", [128, 768], BF16)
        self.SEL2 = self.sb("SEL2", [2, 128], F32)
        self.ONESB = self.sb("ONESB", [128, 128], BF16)
        self.BLK = self.sb("BLK", [128, 128], BF16)
        self.ONEF = self.sb("ONEF", [1, 128], F32)
        self.EPSC = self.sb("EPSC", [128, 1], F32)
        self.NEGC = self.sb("NEGC", [128, 1], F32)
        self.TINY = self.sb("TINY", [1, 8], F32)
        self.ZL = self.sb("ZL", [128, 6, 2], F32)
        self.ZH = self.sb("ZH", [128, 6, 2], F32)
        self.ACC01 = self.sb("ACC01", [128, 6, 2], F32)
        self.B01 = self.sb("B01", [128, 6, 2], F32)
        self.FX = self.sb("FX", [128, 3, 6], F32)
        self.ps = [self.st.enter_context(self.nc.psum_tensor("ps%d" % i, [128, 512], F32)) for i in range(8)]

    def nb(self):
        b = self.bank
        self.bank = (self.bank + 1) % 8
        return b

    def ev_eng(self):
        self.evq += 1
        return "act" if self.evq % 2 else "dve"

    def mm(self, out, lhsT, rhs, start, stop, reads, writes, tp=None):
        if tp is None:
            fn = lambda h: h.matmul(out, lhsT=lhsT, rhs=rhs, start=start, stop=stop)
        else:
            fn = lambda h: h.matmul(out, lhsT=lhsT, rhs=rhs, start=start, stop=stop, tile_position=tp)
        return self.P.add("pe", fn, reads=reads, writes=writes)

    def tr(self, out, in_, reads, writes):
        idn = self.CF[:, 0:128]
        return self.P.add("pe", lambda h: h.transpose(out, in_, idn), reads=list(reads) + ["CF"], writes=writes)

    def act(self, out, in_, func, reads, writes, bias=None, scale=None):
        kw = {}
        if bias is not None:
            kw["bias"] = bias
        if scale is not None:
            kw["scale"] = scale
        return self.P.add("act", lambda h: h.activation(out=out, in_=in_, func=func, **kw), reads=reads, writes=writes)

    def copy(self, eng, out, in_, reads, writes):
        if eng == "act":
            return self.act(out, in_, AF.Copy, reads, writes)
        return self.P.add(eng, lambda h: h.tensor_copy(out=out, in_=in_), reads=reads, writes=writes)

    def tt(self, out, in0, in1, op, reads, writes, eng="dve"):
        return self.P.add(eng, lambda h: h.tensor_tensor(out=out, in0=in0, in1=in1, op=op), reads=reads, writes=writes)

    def stt(self, out, in0, scalar, in1, op0, op1, reads, writes, eng="dve"):
        return self.P.add(eng, lambda h: h.scalar_tensor_tensor(out=out, in0=in0, scalar=scalar, in1=in1, op0=op0, op1=op1),
                          reads=reads, writes=writes)

    def recip(self, out, in_, reads, writes):
        return self.P.add("dve", lambda h: h.reciprocal(out=out, in_=in_), reads=reads, writes=writes)

    def memset(self, ap, val, writes, eng="dve"):
        return self.P.add(eng, lambda h: h.memset(ap, val), writes=writes)

    def dma(self, eng, out, in_, reads, writes, key=None, grp=False):
        def fn(h):
            src = in_() if callable(in_) else in_
            try:
                return h.dma_start(out=out, in_=src)
            except Exception:
                print("DMA FAIL", writes, out, str(src)[:300])
                raise
        return self.P.add(eng, fn, reads=reads, writes=writes, dma=True, key=key, grp=grp)

    def wv(self, s, nk, ncols):
        return self.WS[s][:, 0:nk * ncols].rearrange("p (k n) -> p k n", k=nk)

    def plan_tiles(self):
        plan = []
        for l in self.layers:
            wi = self.w_in[l]

            def cols(c0, n, wi=wi):
                return wi[:, c0:c0 + n].rearrange("(kc k) n -> k kc n", k=128)

            def pre_tiles():
                for t in range(2):
                    plan.append([(16, 384, 0, 384, cols(4096 + 384 * t, 384))])
                for t in range(2):
                    plan.append([(16, 384, 0, 384, cols(4864 + 384 * t, 384))])
                for c in range(6):
                    plan.append([(16, 384, 128 * i, 128, cols(1024 + 768 * i + 128 * c, 128)) for i in range(3)])

            def post_tiles(l=l):
                wo, w1, w2 = self.w_out[l], self.w1[l], self.w2[l]

                def rows(w, r0, nr):
                    return w[r0:r0 + nr * 128, :].rearrange("(rc r) n -> r rc n", r=128)

                for t in range(2):
                    plan.append([(16, 512, 0, 256, cols(256 * t, 256)), (16, 512, 256, 256, cols(512 + 256 * t, 256))])
                plan.append([(4, 2048, 0, 2048, rows(wo, 0, 4))])
                plan.append([(3, 2048, 0, 2048, rows(wo, 512, 3))])
                plan.append([(3, 2048, 0, 2048, rows(wo, 896, 3))])
                for pr in range(2):
                    plan.append([(16, 384, 128 * g, 128, cols(3328 + 128 * (2 * g + pr), 128)) for g in range(3)])
                plan.append([(3, 2048, 0, 2048, rows(wo, 1280, 3))])
                plan.append([(3, 2048, 0, 2048, rows(wo, 1664, 3))])
                seq = [("1", 0)]
                for g in range(16):
                    if g + 1 < 16:
                        seq.append(("1", g + 1))
                    seq.append(("2", g))
                for kind, g in seq:
                    if kind == "1":
                        plan.append([(16, 512, 0, 512, w1[:, 512 * g:512 * g + 512].rearrange("(kc k) n -> k kc n", k=128))])
                    else:
                        plan.append([(4, 2048, 0, 2048, rows(w2, 512 * g, 4))])

            m = self.mode
            if m in ("A", "G"):
                pre_tiles()
            elif m == "B":
                if l == 0:
                    pre_tiles(); post_tiles()
                else:
                    pre_tiles()
            elif m == "C":
                pre_tiles(); post_tiles()
            else:
                pre_tiles(); post_tiles()
        self.wplan = plan

    def issue_tile(self, i):
        s = i % 3
        for (nk, ncols, c0, n, src) in self.wplan[i]:
            dst = self.wv(s, nk, ncols)[:, :, c0:c0 + n]
            self.dma("pool", dst, src, reads=[], writes=[("WS", s)], grp=True)

    def get_tile(self):
        i = self.wcnt
        self.wcnt += 1
        while self.wissued < min(len(self.wplan), i + 3):
            self.issue_tile(self.wissued)
            self.wissued += 1
        return i % 3

    def fm(self, wview, ci, nk, rhs, rhs_res, wres):
        b = [self.nb(), self.nb()]
        for k in range(nk):
            for half in range(2):
                self.mm(self.ps[b[half]][:, :], wview[:, k, ci * 128:(ci + 1) * 128], rhs(k, half),
                        k == 0, k == nk - 1, reads=[wres] + rhs_res(k), writes=[("ps", b[half])])
        return b

    def ht_rhs(self, k, half):
        return self.HT[:, k, half * 512:(half + 1) * 512]

    def ht_res(self, k):
        return [("HT", k)]

    def consts(self):
        self.dma("sp", self.GN[:], self.gn_in, [], ["GN"])
        self.dma("sp", self.CW[:], self.cw_in, [], ["CW"])
        self.dma("sp", self.QKG[:], self.qkg_in, [], ["QKG"])
        self.dma("sp", self.CF[:], self.cf_in, [], ["CF"])
        self.dma("sp", self.CB[:], self.cb_in, [], ["CB"])
        self.dma("sp", self.SEL2[:], self.sel2_in, [], ["SEL2"])
        self.memset(self.ONESB[:], 1.0, ["ONESB"])
        self.memset(self.BLK[:], 0.0, ["BLK"])
        self.memset(self.BLK[0:64, 0:64], 1.0, ["BLK"])
        self.memset(self.BLK[64:128, 64:128], 1.0, ["BLK"])
        self.memset(self.ONEF[:], 1.0, ["ONEF"])
        self.memset(self.EPSC[:], EPS, ["EPSC"])
        self.P.add("dve", lambda h: h.tensor_scalar(out=self.GQ8[:], in0=self.QKG[:, :, 0], scalar1=0.125, scalar2=0.0,
                                                    op0=ALU.mult, op1=ALU.add), reads=["QKG"], writes=["GQ8"])

    def stage(self, sl):
        return self.S[:, 2 * sl:2 * sl + 2, 0:T], [("S", 2 * sl), ("S", 2 * sl + 1)]

    def load_x(self):
        if self.mode == "C":
            for k in range(16):
                self.dma("sp", self.XT[:, k, :], self.xt_in[:, k, :], [], [("XT", k)])
            return
        for tb in range(8):
            sl = tb % 2
            sv, sres = self.stage(sl)
            self.dma("sp", sv, self.x_in[tb * 128:(tb + 1) * 128, :].rearrange("p (a t) -> p a t", a=2), [], sres, key=("Sst", sl))
            for k4 in range(4):
                b = self.nb()
                for j in range(4):
                    k = 4 * k4 + j
                    self.tr(self.ps[b][:, j * 128:(j + 1) * 128], sv[:, k // 8, (k % 8) * 128:(k % 8 + 1) * 128], sres, [("ps", b)])
                self.copy(self.ev_eng(), self.XT[:, 4 * k4:4 * k4 + 4, tb * 128:(tb + 1) * 128],
                          self.ps[b][:, :].rearrange("p (j t) -> p j t", j=4),
                          [("ps", b)], [("XT", 4 * k4 + j) for j in range(4)])

    def store_xt(self):
        for k in range(16):
            o = self.dma("sp", self.xt_out[:, k, :], self.XT[:, k, :], [("XT", k)], [("xt_out", k)])
            self.finals.append(o)

    def store_y(self):
        for tb in range(8):
            sl = tb % 2
            sv, sres = self.stage(sl)
            for k4 in range(4):
                b = self.nb()
                for j in range(4):
                    k = 4 * k4 + j
                    self.tr(self.ps[b][:, j * 128:(j + 1) * 128], self.XT[:, k, tb * 128:(tb + 1) * 128], [("XT", k)], [("ps", b)])
                self.copy(self.ev_eng(), sv[:, k4 // 2, (k4 % 2) * 512:(k4 % 2 + 1) * 512], self.ps[b][:, :], [("ps", b)], sres)
            o = self.dma("sp", self.y_out[tb * 128:(tb + 1) * 128, :].rearrange("p (a t) -> p a t", a=2), sv, sres, [("y", tb)],
                         key=("yst", sl))
            self.finals.append(o)

    def norm(self, l, a):
        for k in range(16):
            self.act(self.HT[:, k, :], self.XT[:, k, :], AF.Square, [("XT", k)], [("HT", k)])
        b = [self.nb(), self.nb()]
        for k in range(16):
            for half in range(2):
                self.mm(self.ps[b[half]][:, :], self.ONESB[:, :], self.HT[:, k, half * 512:(half + 1) * 512], k == 0, k == 15,
                        reads=["ONESB", ("HT", k)], writes=[("ps", b[half])])
        for half in range(2):
            sl = slice(half * 512, (half + 1) * 512)
            self.act(self.RS[:, sl], self.ps[b[half]][:, :], AF.Sqrt, [("ps", b[half]), "EPSC"], [("S", 3)],
                     bias=self.EPSC[:, 0:1], scale=1.0 / D)
            self.recip(self.RS[:, sl], self.RS[:, sl], [("S", 3)], [("S", 3)])
        for k in range(16):
            for half in range(2):
                sl = slice(half * 512, (half + 1) * 512)
                self.stt(self.HT[:, k, sl], self.XT[:, k, sl], self.GN[:, l, a, k:k + 1], self.RS[:, sl], ALU.mult, ALU.mult,
                         [("XT", k), "GN", ("S", 3)], [("HT", k)])

    def headnorm(self, b, gain, out, out_res):
        for half in range(2):
            self.act(self.SQ[:, half * 512:(half + 1) * 512], self.ps[b[half]][:, :], AF.Square, [("ps", b[half])], [("SQ", half)])
        b2 = [self.nb(), self.nb()]
        for half in range(2):
            self.mm(self.ps[b2[half]][:, :], self.BLK[:, :], self.SQ[:, half * 512:(half + 1) * 512], True, True,
                    reads=["BLK", ("SQ", half)], writes=[("ps", b2[half])])
        for half in range(2):
            sl = slice(half * 512, (half + 1) * 512)
            self.act(self.RS[:, sl], self.ps[b2[half]][:, :], AF.Sqrt, [("ps", b2[half]), "EPSC"], [("S", 3)],
                     bias=self.EPSC[:, 0:1], scale=1.0 / 64)
            self.recip(self.RS[:, sl], self.RS[:, sl], [("S", 3)], [("S", 3)])
            self.stt(out[:, sl], self.ps[b[half]][:, :], gain, self.RS[:, sl], ALU.mult, ALU.mult,
                     [("ps", b[half]), ("S", 3), "QKG", "GQ8"], out_res)

    def kv(self, l):
        for t in range(2):
            s = self.get_tile()
            wvw = self.wv(s, 16, 384)
            for j in range(3):
                c = 3 * t + j
                b = self.fm(wvw, j, 16, self.ht_rhs, self.ht_res, ("WS", s))
                kst = c % 2
                self.headnorm(b, self.QKG[:, l, 1:2], self.KST[kst], [("BS", kst)])
                self.dma("sp", self.kx[l][c * 128:(c + 1) * 128, :], self.KST[kst], [("BS", kst)], [("kx", l)], key=("kx", l, kst))
        for t in range(2):
            s = self.get_tile()
            if t == 1 and self.mode in ("F", "G"):
                self.exchange_kv(l, 0)
                self.exchange_kv(l, 1)
            wvw = self.wv(s, 16, 384)
            vres = [("YB", 0, c) for c in range(3)]
            vst = self.YB[0][:, 0:3, :].rearrange("p c t -> p (c t)").rearrange("p (n f) -> p n f", f=384)
            for tb in range(8):
                b = self.nb()
                for k in range(16):
                    self.mm(self.ps[b][:, 0:384], self.HT[:, k, tb * 128:(tb + 1) * 128], wvw[:, k, :], k == 0, k == 15,
                            reads=[("WS", s), ("HT", k)], writes=[("ps", b)])
                self.copy(self.ev_eng(), vst[:, tb, :], self.ps[b][:, 0:384], [("ps", b)], vres)
            self.dma("sp", self.vx[l][:, 384 * t:384 * t + 384].rearrange("(tb p) f -> p tb f", p=128), vst, vres, [("vx", l)],
                     key=("vx", l), grp=True)

    def bmix(self, l):
        ZR = ("S", 2)
        self.memset(self.Z[:, 0:2], 0.0, [ZR])
        for c in range(6):
            s = self.get_tile()
            if c == 1 and self.mode in ("F", "G"):
                self.exchange_kv(l, 2)
                self.exchange_kv(l, 3)
            wvw = self.wv(s, 16, 384)
            bb = self.fm(wvw, 0, 16, self.ht_rhs, self.ht_res, ("WS", s))
            bc = self.fm(wvw, 1, 16, self.ht_rhs, self.ht_res, ("WS", s))
            bx = self.fm(wvw, 2, 16, self.ht_rhs, self.ht_res, ("WS", s))
            for half in range(2):
                sl = slice(half * 512, (half + 1) * 512)
                zsl = slice(2 + half * 512, 2 + (half + 1) * 512)
                self.act(self.S[:, 0, sl], self.ps[bc[half]][:, :], AF.Copy, [("ps", bc[half])], [("S", 0)])
                self.tt(self.Z[:, zsl], self.S[:, 0, sl], self.ps[bx[half]][:, :], ALU.mult, [("S", 0), ("ps", bx[half])], [ZR])
            self.copy("dve", self.ZL[:, c, :], self.Z[:, T:T + 2], [ZR], ["ZL"])
            self.act(self.S[:, 1, 0:T], self.Z[:, 2:T + 2], AF.Copy, [ZR, "CW"], [("S", 1)], scale=self.CW[:, l, c, 2:3])
            self.stt(self.S[:, 1, 0:T], self.Z[:, 1:T + 1], self.CW[:, l, c, 1:2], self.S[:, 1, 0:T], ALU.mult, ALU.add,
                     [ZR, "CW", ("S", 1)], [("S", 1)])
            self.stt(self.S[:, 1, 0:T], self.Z[:, 0:T], self.CW[:, l, c, 0:1], self.S[:, 1, 0:T], ALU.mult, ALU.add,
                     [ZR, "CW", ("S", 1)], [("S", 1)])
            self.copy("dve", self.ACC01[:, c, :], self.S[:, 1, 0:2], [("S", 1)], ["ACC01"])
            self.copy("dve", self.B01[:, c, :], self.ps[bb[0]][:, 0:2], [("ps", bb[0])], ["B01"])
            for half in range(2):
                sl = slice(half * 512, (half + 1) * 512)
                self.tt(self.YB[1][:, c, sl], self.S[:, 1, sl], self.ps[bb[half]][:, :], ALU.mult, [("S", 1), ("ps", bb[half])],
                        [("YB", 1, c)])
        self.dma("sp", self.zx[l].rearrange("(c p) t -> p c t", p=128), self.ZL[:, :, :], ["ZL"], [("zx", l)])

    def fixup(self, l):
        self.load_zh(l)
        fx = self.FX
        zh, cw, a01, b01 = self.ZH, self.CW, self.ACC01, self.B01
        R = ["ZH", "CW", "ACC01", "B01", "FX"]
        self.tt(fx[:, 0, :], zh[:, :, 0], cw[:, l, :, 0], ALU.mult, R, ["FX"])
        self.tt(fx[:, 0, :], fx[:, 0, :], a01[:, :, 0], ALU.add, R, ["FX"])
        self.tt(fx[:, 1, :], zh[:, :, 1], cw[:, l, :, 1], ALU.mult, R, ["FX"])
        self.tt(fx[:, 0, :], fx[:, 0, :], fx[:, 1, :], ALU.add, R, ["FX"])
        self.tt(fx[:, 2, :], zh[:, :, 1], cw[:, l, :, 0], ALU.mult, R, ["FX"])
        self.tt(fx[:, 2, :], fx[:, 2, :], a01[:, :, 1], ALU.add, R, ["FX"])
        yres = [("YB", 1, c) for c in range(6)]
        self.tt(self.YB[1][:, :, 0], fx[:, 0, :], b01[:, :, 0], ALU.mult, R, yres)
        self.tt(self.YB[1][:, :, 1], fx[:, 2, :], b01[:, :, 1], ALU.mult, R, yres)

    def wout(self, l, yb, c0, nr):
        s = self.get_tile()
        wvw = self.wv(s, nr, 2048)
        for dch in range(16):
            b = self.fm(wvw, dch, nr, lambda k, half: self.YB[yb][:, c0 + k, half * 512:(half + 1) * 512],
                        lambda k: [("YB", yb, c0 + k)], ("WS", s))
            for half in range(2):
                sl = slice(half * 512, (half + 1) * 512)
                self.tt(self.XT[:, dch, sl], self.XT[:, dch, sl], self.ps[b[half]][:, :], ALU.add,
                        [("XT", dch), ("ps", b[half])], [("XT", dch)])

    def amix(self, l):
        self.dma("sp", self.SGWT, self.sgwt_in[:, l, :, :], [], [("S", 0)])
        self.dma("sp", self.SGB, self.sgb_in[:, l, :, :], [], [("S", 1)])
        for h_ in range(8):
            self.tt(self.WCT[:, h_, :], self.SGWT[:, h_, :], self.CF[:, 128:256], ALU.mult, [("S", 0), "CF"], ["WCT"])
        VAR = [("BS", 2), ("BS", 3)]
        for t in range(2):
            s = self.get_tile()
            wvw = self.wv(s, 16, 512)
            for tb in range(8):
                b = self.nb()
                for k in range(16):
                    self.mm(self.ps[b][:, 0:256], self.HT[:, k, tb * 128:(tb + 1) * 128], wvw[:, k, 256:512], k == 0, k == 15,
                            reads=[("WS", s), ("HT", k)], writes=[("ps", b)])
                self.copy(self.ev_eng(), self.VA[:, tb, :], self.ps[b][:, 0:256], [("ps", b)], VAR)
            for cc in range(2):
                c = 2 * t + cc
                bu = self.fm(wvw, cc, 16, self.ht_rhs, self.ht_res, ("WS", s))
                for half in range(2):
                    self.act(self.S[:, 2, half * 512:(half + 1) * 512], self.ps[bu[half]][:, :], AF.Copy, [("ps", bu[half])], [("S", 2)])
                bm = [self.nb(), self.nb()]
                for tb in range(8):
                    half, off = tb // 4, (tb % 4) * 128
                    for e in range(2):
                        self.mm(self.ps[bm[half]][64 * e:64 * e + 64, off:off + 128],
                                self.VA[:, tb, cc * 128 + 64 * e:cc * 128 + 64 * e + 64], self.WCT[:, 2 * c + e, :], True, False,
                                reads=VAR + ["WCT"], writes=[("ps", bm[half])], tp=(0, 64 * e))
                    self.mm(self.ps[bm[half]][:, off:off + 128], self.SEL2[:, :], self.SGB[:, c, :], False, True,
                            reads=["SEL2", ("S", 1)], writes=[("ps", bm[half])])
                for half in range(2):
                    sl = slice(half * 512, (half + 1) * 512)
                    self.tt(self.YB[0][:, c, sl], self.S[:, 2, sl], self.ps[bm[half]][:, :], ALU.mult, [("S", 2), ("ps", bm[half])],
                            [("YB", 0, c)])

    def load_zh(self, l):
        if self.mode == "F":
            self.dma("sp", self.ZH[:], self.zwl[l][1].rearrange("(c p) t -> p c t", p=128), [("zwl", l)], ["ZH"])
        else:
            self.dma("sp", self.ZH[:], self.zwin[l].rearrange("(c p) t -> p c t", p=128), [], ["ZH"])

    def load_kwin(self, l, g, kc_, buf):
        d, hp = PATS[g]
        rows = slice(kc_ * 128, (kc_ + 1) * 128)
        KW = self.KWb[buf]
        res = [("HT", 3 * buf + i) for i in range(3)]
        F = self.mode == "F"
        rd = [("kwl", l)] if F else []
        kw = self.kwl[l] if F else self.kwin[l]
        key = ("KW", buf)
        if hp == 2048:
            self.dma("sp", KW[:, 0:T], kw[0, rows, :], rd, res, key=key, grp=True)
            self.dma("sp", KW[:, T:2 * T], kw[1, rows, :], rd, res, key=key, grp=True)
        else:
            self.dma("sp", KW[:, 0:hp], kw[1, rows, T - hp:T], rd, res, key=key, grp=True)
        self.dma("sp", KW[:, hp:hp + T], kw[2, rows, :], rd, res, key=key, grp=True)

    def load_vwin(self, l, g, kc_, buf):
        d, hp = PATS[g]
        L = T // d
        nk = 128 + L
        F = self.mode == "F"
        rd = [("vwl", l)] if F else []
        vw = self.vwl[l] if F else self.vwin[l]
        VW = self.VWb[buf]
        res = [("HT", 6 + 4 * buf + i) for i in range(4)]
        jb = 0
        while jb * 128 < nk:
            nj = min(128, nk - jb * 128)
            w0 = (2048 - hp) + d * 128 * jb
            src = vw[w0:w0 + d * nj, kc_ * 128:(kc_ + 1) * 128].rearrange("(j r) f -> j r f", r=d)
            self.dma("sp", VW[0:nj, jb * d:(jb + 1) * d, :], src, rd, res, key=("VW", buf), grp=True)
            jb += 1

    def attn_prep(self, l):
        tn = self.TINY
        QR = ("S", 3)
        self.dma("sp", self.QKROW, self.qkrow_in[:, l, :], [], [QR])
        R = [QR, "TINY"]
        self.tt(self.QKROW, self.QKROW, self.QKROW, ALU.mult, R, [QR])
        self.P.add("dve", lambda h: h.tensor_reduce(out=tn[:, 0:2], in_=self.QKROW.rearrange("p (a d) -> p a d", a=2),
                                                    axis=AX.X, op=ALU.max), reads=R, writes=["TINY"])
        self.tt(tn[:, 2:3], tn[:, 0:1], tn[:, 1:2], ALU.mult, R, ["TINY"])
        self.act(tn[:, 3:4], tn[:, 2:3], AF.Sqrt, R, ["TINY"], scale=64.0)
        self.P.add("dve", lambda h: h.tensor_scalar(out=tn[:, 4:5], in0=tn[:, 3:4], scalar1=-1.0, scalar2=0.0, op0=ALU.mult, op1=ALU.add),
                   reads=R, writes=["TINY"])
        b = self.nb()
        self.mm(self.ps[b][:, 0:1], self.ONEF[:, :], tn[:, 4:5], True, True, reads=["ONEF", "TINY"], writes=[("ps", b)])
        self.copy("dve", self.NEGC[:, :], self.ps[b][:, 0:1], [("ps", b)], ["NEGC"])

    def attention(self, l):
        self.attn_prep(l)
        self.KWb = [self.HT[:, 3 * i:3 * i + 3, :].rearrange("p c t -> p (c t)") for i in range(2)]
        self.VWb = [self.HT[:, 6 + 4 * i:10 + 4 * i, :].rearrange("p a t -> p (a t)").rearrange("p (n f) -> p n f", f=128) for i in range(2)]
        for pr in range(2):
            s = self.get_tile()
            wvw = self.wv(s, 16, 384)
            for g in range(3):
                b = self.fm(wvw, g, 16, self.ht_rhs, self.ht_res, ("WS", s))
                self.headnorm(b, self.GQ8[:, l:l + 1], self.YB[1][:, 3 * pr + g, :], [("YB", 1, 3 * pr + g)])
        st = dict(etc=0, scq=0, accq=0)
        groups = [(pr, g) for pr in range(2) for g in range(3)]
        items = []

        def mk_load(gi):
            pr, g = groups[gi]
            def back():
                self.load_kwin(l, g, 2 * g + pr, gi % 2)
                self.load_vwin(l, g, 2 * g + pr, gi % 2)
            return dict(front=None, back=back)

        def mk_final(pr):
            def back():
                self.recip(self.S[:, 3, 0:T], self.S[:, 3, 0:T], [("S", 3)], [("S", 3)])
                for g in range(3):
                    c = 2 * g + pr
                    self.tt(self.YB[0][:, c, :], self.S[:, g, 0:T], self.S[:, 3, 0:T], ALU.mult, [("S", g), ("S", 3)], [("YB", 0, c)])
            return dict(front=None, back=back)

        def mk_tile(gi, r, qb0, qb1, bn, bd, e, jb, last):
            pr, g = groups[gi]
            d, hp = PATS[g]
            L = T // d
            qn = min(L, 128)
            nqb = max(1, L // 128)
            qps = nqb // (2 if nqb == 8 else 1)
            buf = gi % 2
            KW, VW = self.KWb[buf], self.VWb[buf]
            KWR = [("HT", 3 * buf + i) for i in range(3)]
            VWR = [("HT", 6 + 4 * buf + i) for i in range(4)]
            qi = 3 * pr + g
            tl = {}

            def front():
                rows = slice(64 * e, 64 * e + 64)
                nj = min(128, 128 + L - jb * 128)
                parts = []
                if qb0 <= jb - 1 <= qb1:
                    parts.append((jb - 1, "hi"))
                if qb0 <= jb <= qb1:
                    parts.append((jb, "lo"))
                nq = qn * len(parts)
                q0 = parts[0][0] * qn
                kcol = slice(d * 128 * jb + r, d * (128 * jb + nj - 1) + r + 1, d)
                qcol = slice(d * q0 + r, d * (q0 + nq - 1) + r + 1, d)
                bs = 4 + st["scq"] % 4
                st["scq"] += 1
                self.mm(self.ps[bs][0:nj, 0:nq], KW[rows, kcol], self.YB[1][rows, qi, qcol], True, True,
                        reads=KWR + [("YB", 1, qi)], writes=[("ps", bs)])
                et = self.ET[st["etc"] % 4]
                eres = [("ET", st["etc"] % 4)]
                st["etc"] += 1
                self.act(et[0:nj, 0:nq], self.ps[bs][0:nj, 0:nq], AF.Exp, [("ps", bs), "NEGC"], eres,
                         bias=self.NEGC[0:nj, 0:1], scale=1.0)
                if jb == 0:
                    mk = self.CB[0:nj, 256 + 128 * g:256 + 128 * g + nq]
                elif len(parts) == 2:
                    mk = self.CB[0:nj, 0:256]
                elif parts[0][1] == "hi":
                    mk = self.CB[0:nj, 0:nq]
                else:
                    mk = self.CB[0:nj, 128:128 + nq]
                self.tt(et[0:nj, 0:nq], et[0:nj, 0:nq], mk, ALU.mult, eres + ["CB"], eres)
                tl.update(nj=nj, parts=parts, et=et, eres=eres, rows=rows)

            def back():
                nj, et, eres, rows = tl["nj"], tl["et"], tl["eres"], tl["rows"]
                for pi, (qb, kind) in enumerate(tl["parts"]):
                    oc = slice((qb - qb0) * qn, (qb - qb0 + 1) * qn)
                    first, lastp = (kind == "lo"), (kind == "hi")
                    rhs = et[0:nj, pi * qn:(pi + 1) * qn]
                    self.mm(self.ps[bn][rows, oc], VW[0:nj, jb * d + r, 64 * e:64 * e + 64], rhs, first, lastp,
                            reads=VWR + eres, writes=[("ps", bn)], tp=(0, 64 * e))
                    self.mm(self.ps[bd][rows, oc], self.ONESB[0:nj, 0:64], rhs, first, lastp,
                            reads=["ONESB"] + eres, writes=[("ps", bd)], tp=(0, 64 * e))
                if last:
                    ntok = qps * qn
                    i0 = qb0 * qn
                    tcol = slice(d * i0 + r, d * (i0 + ntok - 1) + r + 1, d)
                    self.act(self.S[:, g, tcol], self.ps[bn][:, 0:ntok], AF.Copy, [("ps", bn)], [("S", g)])
                    if g == 0:
                        self.copy("dve", self.S[:, 3, tcol], self.ps[bd][:, 0:ntok], [("ps", bd)], [("S", 3)])
                    else:
                        self.tt(self.S[:, 3, tcol], self.S[:, 3, tcol], self.ps[bd][:, 0:ntok], ALU.add, [("S", 3), ("ps", bd)],
                                [("S", 3)])
            return dict(front=front, back=back)

        mk_load(0)["back"]()
        mk_load(1)["back"]()
        for gi, (pr, g) in enumerate(groups):
            d, hp = PATS[g]
            L = T // d
            nqb = max(1, L // 128)
            nsup = 2 if nqb == 8 else 1
            qps = nqb // nsup
            if gi >= 1 and gi + 1 < len(groups):
                items.append(mk_load(gi + 1))
            for r in range(d):
                for sp_ in range(nsup):
                    qb0, qb1 = sp_ * qps, sp_ * qps + qps - 1
                    bn, bd = (0, 1) if st["accq"] % 2 == 0 else (2, 3)
                    st["accq"] += 1
                    gt = [(e, jb) for e in range(2) for jb in range(qb0, qb1 + 2)]
                    for ti, (e, jb) in enumerate(gt):
                        items.append(mk_tile(gi, r, qb0, qb1, bn, bd, e, jb, ti == len(gt) - 1))
            if g == 2:
                items.append(mk_final(pr))
        LA = 3
        for i in range(len(items) + LA):
            if i < len(items) and items[i]["front"] is not None:
                items[i]["front"]()
            if i >= LA:
                items[i - LA]["back"]()

    def mlp(self, l):
        self.norm(l, 1)

        def w1stage(g):
            s = self.get_tile()
            wvw = self.wv(s, 16, 512)
            hb = g % 2
            for f in range(4):
                b = self.fm(wvw, f, 16, self.ht_rhs, self.ht_res, ("WS", s))
                for half in range(2):
                    rt = self.RT[half]
                    self.act(rt, self.ps[b[half]][:, :], AF.Relu, [("ps", b[half])], [("S", half)])
                    self.act(self.YB[hb][:, f, half * 512:(half + 1) * 512], rt, AF.Square, [("S", half)], [("YB", hb, f)])

        def w2stage(g):
            self.wout(l, g % 2, 0, 4)

        w1stage(0)
        for g in range(16):
            if g + 1 < 16:
                w1stage(g + 1)
            w2stage(g)

    def zero_pads(self):
        zres = [("YB", 0, c) for c in range(6)]
        self.memset(self.YB[0][:, :, :], 0.0, zres)
        self.memset(self.FX[:, :, :], 0.0, ["FX"])
        zsrc = self.YB[0][:, :, :].rearrange("p c t -> p (c t)")
        for l in self.layers:
            for h_ in range(2):
                self.dma("sp", self.kwin[l][h_][0:2].rearrange("b (p a) t -> p b (a t)", p=128),
                         zsrc.rearrange("p (b x) -> p b x", b=2), zres, [("kgpad", l, h_)])
                self.dma("sp", self.vwin[l][h_][0:2].rearrange("b (p a) f -> p b (a f)", p=128),
                         zsrc.rearrange("p (b x) -> p b x", b=2), zres, [("vgpad", l, h_)])
            for bi in range(2):
                self.dma("sp", self.zwin[l][bi].rearrange("(p a) t -> p (a t)", p=128),
                         self.FX[:, :, :].rearrange("p a b -> p (a b)")[:, 0:12], ["FX"], [("zgpad", l, bi)])

        def setrv(h):
            r = h.partition_id() % 4
            self.rvb = [h.snap(r, min_val=0, max_val=3)]
            return None
        self.P.add("sp", setrv, reads=[], writes=["rv"], nosync=True)

    def _cc(self, src, dst, rres, wres, key):
        groups = [[0, 1, 2, 3], [4, 5, 6, 7]]
        self.P.add("pool", lambda h: h.collective_compute("AllGather", ALU.bypass, replica_groups=groups, ins=[src], outs=[dst]),
                   reads=rres, writes=wres, dma=True, key=key, inc=self.cc_inc)

    def exchange_kv(self, l, i):
        h_ = i % 2
        if i < 2:
            self._cc(self.kx[l][384 * h_:384 * h_ + 384, :], self.kwin[l][h_][2:6].rearrange("b f t -> (b f) t"),
                     [("kx", l), ("kgpad", l, h_)], [("kg", l, h_)], ("cck", l, h_))
        else:
            self._cc(self.vx[l][512 * h_:512 * h_ + 512, :], self.vwin[l][h_][2:6].rearrange("b t f -> (b t) f"),
                     [("vx", l), ("vgpad", l, h_)], [("vg", l, h_)], ("ccv", l, h_))

    def exchange_z(self, l):
        self._cc(self.zx[l], self.zwin[l][2:6].rearrange("b f t -> (b f) t"), [("zx", l), ("zgpad", l, 0), ("zgpad", l, 1)],
                 [("zg", l)], ("ccz", l))

    def localize(self, l):
        v3 = self.vwl[l].rearrange("(b t) f -> b t f", b=3)
        for h_ in range(2):
            self.dma("sp", self.kwl[l][:, 384 * h_:384 * h_ + 384, :],
                     (lambda h_=h_: self.kwin[l][h_][bass.ds(self.rvb[0], 3), :, :]), [("kg", l, h_), "rv"], [("kwl", l)], key=("kwl", l), grp=True)
            self.dma("sp", v3[:, 512 * h_:512 * h_ + 512, :],
                     (lambda h_=h_: self.vwin[l][h_][bass.ds(self.rvb[0], 3), :, :]), [("vg", l, h_), "rv"], [("vwl", l)], key=("vwl", l), grp=True)
        self.dma("sp", self.zwl[l].rearrange("b f t -> (b f) t"),
                 lambda: self.zwin[l][bass.ds(self.rvb[0], 3), :, :].rearrange("b f t -> (b f) t"), [("zg", l), "rv"], [("zwl", l)])

    def pre(self, l, do_kv=True):
        self.norm(l, 0)
        if do_kv:
            self.kv(l)
        else:
            for _ in range(4):
                self.get_tile()
        self.bmix(l)
        if self.mode in ("F", "G"):
            self.exchange_z(l)

    def post(self, l):
        self.amix(l)
        self.wout(l, 0, 0, 4)
        if self.mode == "F":
            self.localize(l)
        self.fixup(l)
        self.wout(l, 1, 0, 3)
        self.wout(l, 1, 3, 3)
        self.attention(l)
        self.wout(l, 0, 0, 3)
        self.wout(l, 0, 3, 3)
        self.mlp(l)

    def build(self, cc_inc=1):
        m = self.mode
        self.cc_inc = cc_inc
        self.plan_tiles()
        self.consts()
        self.load_x()
        if m == "F":
            self.zero_pads()
        if m == "A":
            self.pre(0)
            for key in (("kx", 0), ("vx", 0), ("zx", 0)):
                self.finals.append(self.P.last_w[key])
        elif m == "B":
            self.pre(0)
            self.post(0)
            self.pre(1)
            for key in (("kx", 1), ("vx", 1), ("zx", 1)):
                self.finals.append(self.P.last_w[key])
            self.store_xt()
        elif m == "C":
            self.pre(1)
            self.post(1)
            self.store_y()
        elif m == "G":
            self.zero_pads()
            self.pre(0)
            self.localize(0)
            self.gk = self.dram_out("gk", [3, 768, T], BF16)
            self.gv = self.dram_out("gv", [3 * T, 768], BF16)
            self.gz = self.dram_out("gz", [3, 768, 2], F32)
            self.finals.append(self.dma("sp", self.gk.rearrange("b f t -> (b f) t"), self.kwl[0].rearrange("b f t -> (b f) t"), [("kwl", 0)], ["gk"]))
            self.finals.append(self.dma("sp", self.gv, self.vwl[0], [("vwl", 0)], ["gv"]))
            self.finals.append(self.dma("sp", self.gz.rearrange("b f t -> (b f) t"), self.zwl[0].rearrange("b f t -> (b f) t"), [("zwl", 0)], ["gz"]))
        else:
            for l in (0, 1):
                self.pre(l)
                self.post(l)
            self.store_y()
        with self.nc.allow_non_contiguous_dma(reason="tiny halo / param layouts"):
            fin = list(self.finals)
            for e in ENGS:
                if self.P.q[e]:
                    fin.append(self.P.q[e][-1])
            self.P.emit(final_waits=fin)
        self.st.close()
        return self.nc


def _consts(rank):
    jj = np.arange(128)[:, None]
    ii = np.arange(128)[None, :]
    m_hi = (jj <= ii).astype(np.float32)
    m_lo = (jj >= ii).astype(np.float32)
    cf = np.concatenate([np.eye(128, dtype=np.float32), m_hi], axis=1)
    hm = []
    for g in range(3):
        if g < 2:
            valid = np.full((128, 1), 1.0 if rank >= 1 else 0.0, np.float32)
        else:
            valid = np.zeros((128, 1), np.float32)
            if rank >= 2:
                valid[:] = 1.0
            elif rank == 1:
                valid[64:] = 1.0
        hm.append(m_lo * valid)
    cb = np.concatenate([m_hi, m_lo] + hm + [np.eye(128, dtype=np.float32)], axis=1).astype(NPBF)
    sel2 = np.zeros((2, 128), np.float32)
    sel2[0, :64] = 1.0
    sel2[1, 64:] = 1.0
    return cf, cb, sel2


def _params(inp):
    gn = np.stack([inp["attn_norm"], inp["mlp_norm"]], axis=1)
    gn = np.ascontiguousarray(gn.reshape(2, 2, 16, 128).transpose(3, 0, 1, 2))
    cw = np.ascontiguousarray(inp["conv_w"].reshape(2, 3, 6, 128).transpose(3, 0, 2, 1))
    qk = np.stack([inp["q_norm"], inp["k_norm"]], axis=2)
    qkg = np.ascontiguousarray(np.concatenate([qk, qk], axis=1).transpose(1, 0, 2))
    qkrow = np.ascontiguousarray(np.concatenate([inp["q_norm"], inp["k_norm"]], axis=1)[None])
    sgwt = np.ascontiguousarray(inp["sgu_w"].transpose(3, 0, 1, 2))
    sgb = np.ascontiguousarray(inp["sgu_b"].reshape(2, 4, 2, 128).transpose(2, 0, 1, 3))
    return dict(gn=gn, cw=cw, qkg=qkg, qkrow=qkrow, sgwt=sgwt, sgb=sgb)


_CACHE = {}


def _get_nc(mode, **kw):
    key = (mode, tuple(sorted(kw.items())))
    if key not in _CACHE:
        _CACHE[key] = Builder(mode).build(**kw)
    return _CACHE[key]


def _windows(kxs, vxs, zxs):
    outs = []
    for c in range(8):
        r = c % 4
        kw = np.zeros((3, 768, T), NPBF)
        vw = np.zeros((3 * T, 768), NPBF)
        zw = np.zeros((768, 2), np.float32)
        for bi in range(3):
            src = r - 2 + bi
            if src >= 0:
                kw[bi] = kxs[c - r + src]
                vw[bi * T:(bi + 1) * T] = vxs[c - r + src]
        if r >= 1:
            zw = zxs[c - 1]
        outs.append((kw, vw, zw))
    return outs


FUSED = True


def kernel(**inp):
    inp = {k: np.asarray(v) for k, v in inp.items()}
    x = np.ascontiguousarray(inp["x"].reshape(8, T, D))
    prm = _params(inp)
    base = []
    for c in range(8):
        cf, cb, sel2 = _consts(c % 4)
        d = dict(prm)
        d.update(cf=cf, cb=cb, sel2=sel2)
        base.append(d)
    W = {}
    for l in range(2):
        W["w_in%d" % l] = np.ascontiguousarray(inp["w_in"][l])
        W["w_out%d" % l] = np.ascontiguousarray(inp["w_out"][l])
        W["w1_%d" % l] = np.ascontiguousarray(inp["w_mlp_in"][l])
        W["w2_%d" % l] = np.ascontiguousarray(inp["w_mlp_out"][l])
    cores = list(range(8))
    if FUSED:
        ncF = _get_nc("F")
        mapsF = [dict(base[c], x=x[c], **W) for c in cores]
        rF = run_bass_kernel_spmd(ncF, mapsF, core_ids=cores).results
        y = np.stack([rF[c]["y"] for c in cores], axis=0).reshape(2, 4 * T, D)
        return y.astype(np.float32)
    return _kernel_unfused(x, base, W)


def _kernel_unfused(x, base, W):
    cores = list(range(8))
    ncA = _get_nc("A")
    mapsA = [dict(base[c], x=x[c], w_in0=W["w_in0"]) for c in cores]
    rA = run_bass_kernel_spmd(ncA, mapsA, core_ids=cores).results
    win0 = _windows([r["kx0"] for r in rA], [r["vx0"] for r in rA], [r["zx0"] for r in rA])
    ncB = _get_nc("B")
    mapsB = [dict(base[c], x=x[c], w_in0=W["w_in0"], w_out0=W["w_out0"], w1_0=W["w1_0"], w2_0=W["w2_0"], w_in1=W["w_in1"],
                  kwin0=win0[c][0], vwin0=win0[c][1], zwin0=win0[c][2]) for c in cores]
    rB = run_bass_kernel_spmd(ncB, mapsB, core_ids=cores).results
    win1 = _windows([r["kx1"] for r in rB], [r["vx1"] for r in rB], [r["zx1"] for r in rB])
    ncC = _get_nc("C")
    mapsC = [dict(base[c], xt_in=rB[c]["xt_out"], w_in1=W["w_in1"], w_out1=W["w_out1"], w1_1=W["w1_1"], w2_1=W["w2_1"],
                  kwin1=win1[c][0], vwin1=win1[c][1], zwin1=win1[c][2]) for c in cores]
    rC = run_bass_kernel_spmd(ncC, mapsC, core_ids=cores).results
    y = np.stack([rC[c]["y"] for c in cores], axis=0).reshape(2, 4 * T, D)
    return y.astype(np.float32)
```

```python
import contextlib
import numpy as np
import ml_dtypes
import concourse.bass as bass
import concourse.mybir as mybir
from concourse.bass_utils import run_bass_kernel_spmd

F32 = mybir.dt.float32
BF16 = mybir.dt.bfloat16
I32 = mybir.dt.int32
AF = mybir.ActivationFunctionType
ALU = mybir.AluOpType
AX = mybir.AxisListType
NPBF = ml_dtypes.bfloat16

ENGS = ("pe", "act", "dve", "pool", "sp")
T = 1024
D = 2048
DIN = 5632
DFF = 8192
EPS = 1e-6
PATS = ((1, 128), (4, 512), (16, 2048))


class Op:
    __slots__ = ("eng", "fn", "deps", "dma", "key", "ms", "has_cons", "gidx", "inc", "nosync")

    def __init__(self, eng, fn, dma, key, gidx, inc):
        self.eng = eng
        self.fn = fn
        self.deps = []
        self.dma = dma
        self.key = key
        self.ms = None
        self.has_cons = False
        self.gidx = gidx
        self.inc = inc
        self.nosync = False


class Prog:
    def __init__(self, nc, same_engine_sync=True):
        self.nc = nc
        self.q = {e: [] for e in ENGS}
        self.last_w = {}
        self.readers = {}
        self.n = 0
        self.same_engine_sync = same_engine_sync

    def add(self, eng, fn, reads=(), writes=(), dma=False, key=None, inc=None, grp=False, nosync=False):
        if dma and key is None:
            key = writes[0]
        op = Op(eng, fn, dma, key, self.n, inc if inc is not None else (16 if dma else 1))
        op.nosync = nosync
        self.n += 1
        deps = {}
        for r in reads:
            w = self.last_w.get(r)
            if w is not None:
                deps[id(w)] = w
        for w_ in writes:
            w = self.last_w.get(w_)
            if w is not None and not (grp and w.dma and w.key == key):
                deps[id(w)] = w
            for rd in self.readers.get(w_, ()):
                deps[id(rd)] = rd
        op.deps = list(deps.values())
        for d in op.deps:
            d.has_cons = True
        for r in reads:
            self.readers.setdefault(r, []).append(op)
        for w_ in writes:
            self.last_w[w_] = op
            self.readers[w_] = []
        self.q[eng].append(op)
        return op

    def emit(self, final_waits=()):
        nc = self.nc
        eng_cnt = {e: 0 for e in ENGS}
        key_cnt = {}
        allops = sorted([o for e in ENGS for o in self.q[e]], key=lambda o: o.gidx)
        for o in final_waits:
            o.has_cons = True
        for o in allops:
            if o.dma:
                key_cnt[o.key] = key_cnt.get(o.key, 0) + o.inc
                o.ms = key_cnt[o.key]
            elif o.has_cons and not o.nosync:
                eng_cnt[o.eng] += 1
                o.ms = eng_cnt[o.eng]
        keys = sorted(key_cnt.keys(), key=str)
        with contextlib.ExitStack() as st:
            esem = {e: st.enter_context(nc.semaphore("s_" + e)) for e in ENGS}
            ksem = {k: st.enter_context(nc.semaphore("k%d" % i)) for i, k in enumerate(keys)}
            block = st.enter_context(nc.Block())
            self.nsem = len(esem) + len(ksem)

            def run(ename, h):
                waited = {}
                for o in self.q[ename]:
                    for d in o.deps:
                        if d.nosync:
                            continue
                        if d.dma:
                            s, v = ksem[d.key], d.ms
                        else:
                            if d.eng == ename and (ename == "pe" or not self.same_engine_sync):
                                continue
                            s, v = esem[d.eng], d.ms
                        if waited.get(id(s), 0) >= v:
                            continue
                        waited[id(s)] = v
                        h.wait_ge(s, v)
                    ins = o.fn(h)
                    if o.dma:
                        ins.then_inc(ksem[o.key], o.inc)
                    elif o.has_cons and not o.nosync:
                        ins.then_inc(esem[o.eng], 1)
                if ename == "sp":
                    for d in final_waits:
                        if d.nosync:
                            continue
                        s, v = (ksem[d.key], d.ms) if d.dma else (esem[d.eng], d.ms)
                        h.wait_ge(s, v)

            @block.tensor
            def _(h):
                run("pe", h)

            @block.scalar
            def _(h):
                run("act", h)

            @block.vector
            def _(h):
                run("dve", h)

            @block.gpsimd
            def _(h):
                run("pool", h)

            @block.sync
            def _(h):
                run("sp", h)


class Builder:
    def __init__(self, mode):
        self.mode = mode
        nc = self.nc = bass.Bass("TRN2", target_bir_lowering=False)
        self.st = contextlib.ExitStack()
        self.P = Prog(nc)
        self.bank = 0
        self.wcnt = 0
        self.wissued = 0
        self.wplan = []
        self.finals = []
        self.evq = 0
        self.layers = {"A": [0], "B": [0, 1], "C": [1], "F": [0, 1], "G": [0]}[mode]
        self._declare()

    def dram_in(self, name, shape, dt):
        return self.nc.dram_tensor(name, list(shape), dt, kind="ExternalInput").ap()

    def dram_out(self, name, shape, dt):
        return self.nc.dram_tensor(name, list(shape), dt, kind="ExternalOutput").ap()

    def dram_int(self, name, shape, dt):
        return self.nc.dram_tensor(name, list(shape), dt).ap()

    def sb(self, name, shape, dt):
        return self.st.enter_context(self.nc.sbuf_tensor(name, list(shape), dt))

    def _declare(self):
        m = self.mode
        if m in ("A", "B", "F", "G"):
            self.x_in = self.dram_in("x", [T, D], F32)
        if m == "C":
            self.xt_in = self.dram_in("xt_in", [128, 16, T], F32)
        if m == "B":
            self.xt_out = self.dram_out("xt_out", [128, 16, T], F32)
        if m in ("C", "F"):
            self.y_out = self.dram_out("y", [T, D], F32)
        self.w_in = {}
        self.w_out = {}
        self.w1 = {}
        self.w2 = {}
        for l in self.layers:
            self.w_in[l] = self.dram_in("w_in%d" % l, [D, DIN], F32)
            full = not (m in ("A", "G") or (m == "B" and l == 1))
            if full:
                self.w_out[l] = self.dram_in("w_out%d" % l, [D, D], F32)
                self.w1[l] = self.dram_in("w1_%d" % l, [D, DFF], F32)
                self.w2[l] = self.dram_in("w2_%d" % l, [DFF, D], F32)
        self.gn_in = self.dram_in("gn", [128, 2, 2, 16], F32)
        self.cw_in = self.dram_in("cw", [128, 2, 6, 3], F32)
        self.qkg_in = self.dram_in("qkg", [128, 2, 2], F32)
        self.qkrow_in = self.dram_in("qkrow", [1, 2, 128], F32)
        self.sgwt_in = self.dram_in("sgwt", [128, 2, 8, 128], F32)
        self.sgb_in = self.dram_in("sgb", [2, 2, 4, 128], F32)
        self.cf_in = self.dram_in("cf", [128, 256], F32)
        self.cb_in = self.dram_in("cb", [128, 768], BF16)
        self.sel2_in = self.dram_in("sel2", [2, 128], F32)
        self.kx = {}
        self.vx = {}
        self.zx = {}
        self.kwin = {}
        self.vwin = {}
        self.zwin = {}
        for l in self.layers:
            export = (m == "A" and l == 0) or (m == "B" and l == 1)
            mk = self.dram_out if export else self.dram_int
            if m not in ("F", "G"):
                self.kx[l] = mk("kx%d" % l, [768, T], BF16)
                self.vx[l] = mk("vx%d" % l, [T, 768], BF16)
                self.zx[l] = mk("zx%d" % l, [768, 2], F32)
            post = (m == "B" and l == 0) or (m == "C" and l == 1)
            if post:
                self.kwin[l] = self.dram_in("kwin%d" % l, [3, 768, T], BF16)
                self.vwin[l] = self.dram_in("vwin%d" % l, [3 * T, 768], BF16)
                self.zwin[l] = self.dram_in("zwin%d" % l, [768, 2], F32)
        if m in ("F", "G"):
            for l in self.layers:
                self.kx[l] = self.dram_int("kx%d" % l, [768, T], BF16)
                self.vx[l] = self.dram_int("vx%d" % l, [T, 768], BF16)
                self.zx[l] = self.dram_int("zx%d" % l, [768, 2], F32)
                self.kwin[l] = [self.dram_int("kg%d_%d" % (l, h_), [6, 384, T], BF16) for h_ in range(2)]
                self.vwin[l] = [self.dram_int("vg%d_%d" % (l, h_), [6, 512, 768], BF16) for h_ in range(2)]
                self.zwin[l] = self.dram_int("zg%d" % l, [6, 768, 2], F32)
            self.kwl = {l: self.dram_int("kwl%d" % l, [3, 768, T], BF16) for l in self.layers}
            self.vwl = {l: self.dram_int("vwl%d" % l, [3 * T, 768], BF16) for l in self.layers}
            self.zwl = {l: self.dram_int("zwl%d" % l, [3, 768, 2], F32) for l in self.layers}
        self.XT = self.sb("XT", [128, 16, T], F32)
        self.HT = self.sb("HT", [128, 16, T], BF16)
        self.WS = [self.sb("WS%d" % i, [128, 8192], BF16) for i in range(3)]
        self.YB = [self.sb("YB%d" % i, [128, 6, T], BF16) for i in range(2)]
        self.S = self.sb("S", [128, 4, 1032], F32)
        self.BS = self.sb("BS", [128, 6, T], BF16)
        self.Z = self.S[:, 2, 0:T + 2]
        self.RS = self.S[:, 3, 0:T]
        self.SQ = self.BS[:, 4, :]
        self.KST = [self.BS[:, 0, :], self.BS[:, 1, :]]
        self.ET = [self.BS[:, 5, i * 256:(i + 1) * 256] for i in range(4)]
        self.VA = self.BS[:, 2:4, :].rearrange("p a t -> p (a t)").rearrange("p (n f) -> p n f", f=256)
        self.RT = [self.S[:, 0, 0:512], self.S[:, 1, 0:512]]
        self.GN = self.sb("GN", [128, 2, 2, 16], F32)
        self.CW = self.sb("CW", [128, 2, 6, 3], F32)
        self.QKG = self.sb("QKG", [128, 2, 2], F32)
        self.GQ8 = self.sb("GQ8", [128, 2], F32)
        self.QKROW = self.S[0:1, 3, 0:128]
        self.SGWT = self.S[:, 0, 0:T].rearrange("p (h i) -> p h i", h=8)
        self.WCT = self.sb("WCT", [128, 8, 128], BF16)
        self.SGB = self.S[0:2, 1, 0:512].rearrange("p (c i) -> p c i", c=4)
        self.CF = self.sb("CF", [128, 256], F32)
        self.CB = self.sb("CB", [128, 768], BF16)
        self.SEL2 = self.sb("SEL2", [2, 128], F32)
        self.ONESB = self.sb("ONESB", [128, 128], BF16)
        self.BLK = self.sb("BLK", [128, 128], BF16)
        self.ONEF = self.sb("ONEF", [1, 128], F32)
        self.EPSC = self.sb("EPSC", [128, 1], F32)
        self.NEGC = self.sb("NEGC", [128, 1], F32)
        self.TINY = self.sb("TINY", [1, 8], F32)
        self.ZL = self.sb("ZL", [128, 6, 2], F32)
        self.ZH = self.sb("ZH", [128, 6, 2], F32)
        self.ACC01 = self.sb("ACC01", [128, 6, 2], F32)
        self.B01 = self.sb("B01", [128, 6, 2], F32)
        self.FX = self.sb("FX", [128, 3, 6], F32)
        self.ps = [self.st.enter_context(self.nc.psum_tensor("ps%d" % i, [128, 512], F32)) for i in range(8)]

    def nb(self):
        b = self.bank
        self.bank = (self.bank + 1) % 8
        return b

    def ev_eng(self):
        self.evq += 1
        return "act" if self.evq % 2 else "dve"

    def mm(self, out, lhsT, rhs, start, stop, reads, writes, tp=None):
        if tp is None:
            fn = lambda h: h.matmul(out, lhsT=lhsT, rhs=rhs, start=start, stop=stop)
        else:
            fn = lambda h: h.matmul(out, lhsT=lhsT, rhs=rhs, start=start, stop=stop, tile_position=tp)
        return self.P.add("pe", fn, reads=reads, writes=writes)

    def tr(self, out, in_, reads, writes):
        idn = self.CF[:, 0:128]
        return self.P.add("pe", lambda h: h.transpose(out, in_, idn), reads=list(reads) + ["CF"], writes=writes)

    def act(self, out, in_, func, reads, writes, bias=None, scale=None):
        kw = {}
        if bias is not None:
            kw["bias"] = bias
        if scale is not None:
            kw["scale"] = scale
        return self.P.add("act", lambda h: h.activation(out=out, in_=in_, func=func, **kw), reads=reads, writes=writes)

    def copy(self, eng, out, in_, reads, writes):
        if eng == "act":
            return self.act(out, in_, AF.Copy, reads, writes)
        return self.P.add(eng, lambda h: h.tensor_copy(out=out, in_=in_), reads=reads, writes=writes)

    def tt(self, out, in0, in1, op, reads, writes, eng="dve"):
        return self.P.add(eng, lambda h: h.tensor_tensor(out=out, in0=in0, in1=in1, op=op), reads=reads, writes=writes)

    def stt(self, out, in0, scalar, in1, op0, op1, reads, writes, eng="dve"):
        return self.P.add(eng, lambda h: h.scalar_tensor_tensor(out=out, in0=in0, scalar=scalar, in1=in1, op0=op0, op1=op1),
                          reads=reads, writes=writes)

    def recip(self, out, in_, reads, writes):
        return self.P.add("dve", lambda h: h.reciprocal(out=out, in_=in_), reads=reads, writes=writes)

    def memset(self, ap, val, writes, eng="dve"):
        return self.P.add(eng, lambda h: h.memset(ap, val), writes=writes)

    def dma(self, eng, out, in_, reads, writes, key=None, grp=False):
        def fn(h):
            src = in_() if callable(in_) else in_
            try:
                return h.dma_start(out=out, in_=src)
            except Exception:
                print("DMA FAIL", writes, out, str(src)[:300])
                raise
        return self.P.add(eng, fn, reads=reads, writes=writes, dma=True, key=key, grp=grp)

    def wv(self, s, nk, ncols):
        return self.WS[s][:, 0:nk * ncols].rearrange("p (k n) -> p k n", k=nk)

    def plan_tiles(self):
        plan = []
        for l in self.layers:
            wi = self.w_in[l]

            def cols(c0, n, wi=wi):
                return wi[:, c0:c0 + n].rearrange("(kc k) n -> k kc n", k=128)

            def pre_tiles():
                for t in range(2):
                    plan.append([(16, 384, 0, 384, cols(4096 + 384 * t, 384))])
                for t in range(2):
                    plan.append([(16, 384, 0, 384, cols(4864 + 384 * t, 384))])
                for c in range(6):
                    plan.append([(16, 384, 128 * i, 128, cols(1024 + 768 * i + 128 * c, 128)) for i in range(3)])

            def post_tiles(l=l):
                wo, w1, w2 = self.w_out[l], self.w1[l], self.w2[l]

                def rows(w, r0, nr):
                    return w[r0:r0 + nr * 128, :].rearrange("(rc r) n -> r rc n", r=128)

                for t in range(2):
                    plan.append([(16, 512, 0, 256, cols(256 * t, 256)), (16, 512, 256, 256, cols(512 + 256 * t, 256))])
                plan.append([(4, 2048, 0, 2048, rows(wo, 0, 4))])
                plan.append([(3, 2048, 0, 2048, rows(wo, 512, 3))])
                plan.append([(3, 2048, 0, 2048, rows(wo, 896, 3))])
                for pr in range(2):
                    plan.append([(16, 384, 128 * g, 128, cols(3328 + 128 * (2 * g + pr), 128)) for g in range(3)])
                plan.append([(3, 2048, 0, 2048, rows(wo, 1280, 3))])
                plan.append([(3, 2048, 0, 2048, rows(wo, 1664, 3))])
                seq = [("1", 0)]
                for g in range(16):
                    if g + 1 < 16:
                        seq.append(("1", g + 1))
                    seq.append(("2", g))
                for kind, g in seq:
                    if kind == "1":
                        plan.append([(16, 512, 0, 512, w1[:, 512 * g:512 * g + 512].rearrange("(kc k) n -> k kc n", k=128))])
                    else:
                        plan.append([(4, 2048, 0, 2048, rows(w2, 512 * g, 4))])

            m = self.mode
            if m in ("A", "G"):
                pre_tiles()
            elif m == "B":
                if l == 0:
                    pre_tiles(); post_tiles()
                else:
                    pre_tiles()
            elif m == "C":
                pre_tiles(); post_tiles()
            else:
                pre_tiles(); post_tiles()
        self.wplan = plan

    def issue_tile(self, i):
        s = i % 3
        for (nk, ncols, c0, n, src) in self.wplan[i]:
            dst = self.wv(s, nk, ncols)[:, :, c0:c0 + n]
            self.dma("pool", dst, src, reads=[], writes=[("WS", s)], grp=True)

    def get_tile(self):
        i = self.wcnt
        self.wcnt += 1
        while self.wissued < min(len(self.wplan), i + 3):
            self.issue_tile(self.wissued)
            self.wissued += 1
        return i % 3

    def fm(self, wview, ci, nk, rhs, rhs_res, wres):
        b = [self.nb(), self.nb()]
        for k in range(nk):
            for half in range(2):
                self.mm(self.ps[b[half]][:, :], wview[:, k, ci * 128:(ci + 1) * 128], rhs(k, half),
                        k == 0, k == nk - 1, reads=[wres] + rhs_res(k), writes=[("ps", b[half])])
        return b

    def ht_rhs(self, k, half):
        return self.HT[:, k, half * 512:(half + 1) * 512]

    def ht_res(self, k):
        return [("HT", k)]

    def consts(self):
        self.dma("sp", self.GN[:], self.gn_in, [], ["GN"])
        self.dma("sp", self.CW[:], self.cw_in, [], ["CW"])
        self.dma("sp", self.QKG[:], self.qkg_in, [], ["QKG"])
        self.dma("sp", self.CF[:], self.cf_in, [], ["CF"])
        self.dma("sp", self.CB[:], self.cb_in, [], ["CB"])
        self.dma("sp", self.SEL2[:], self.sel2_in, [], ["SEL2"])
        self.memset(self.ONESB[:], 1.0, ["ONESB"])
        self.memset(self.BLK[:], 0.0, ["BLK"])
        self.memset(self.BLK[0:64, 0:64], 1.0, ["BLK"])
        self.memset(self.BLK[64:128, 64:128], 1.0, ["BLK"])
        self.memset(self.ONEF[:], 1.0, ["ONEF"])
        self.memset(self.EPSC[:], EPS, ["EPSC"])
        self.P.add("dve", lambda h: h.tensor_scalar(out=self.GQ8[:], in0=self.QKG[:, :, 0], scalar1=0.125, scalar2=0.0,
                                                    op0=ALU.mult, op1=ALU.add), reads=["QKG"], writes=["GQ8"])

    def stage(self, sl):
        return self.S[:, 2 * sl:2 * sl + 2, 0:T], [("S", 2 * sl), ("S", 2 * sl + 1)]

    def load_x(self):
        if self.mode == "C":
            for k in range(16):
                self.dma("sp", self.XT[:, k, :], self.xt_in[:, k, :], [], [("XT", k)])
            return
        for tb in range(8):
            sl = tb % 2
            sv, sres = self.stage(sl)
            self.dma("sp", sv, self.x_in[tb * 128:(tb + 1) * 128, :].rearrange("p (a t) -> p a t", a=2), [], sres, key=("Sst", sl))
            for k4 in range(4):
                b = self.nb()
                for j in range(4):
                    k = 4 * k4 + j
                    self.tr(self.ps[b][:, j * 128:(j + 1) * 128], sv[:, k // 8, (k % 8) * 128:(k % 8 + 1) * 128], sres, [("ps", b)])
                self.copy(self.ev_eng(), self.XT[:, 4 * k4:4 * k4 + 4, tb * 128:(tb + 1) * 128],
                          self.ps[b][:, :].rearrange("p (j t) -> p j t", j=4),
                          [("ps", b)], [("XT", 4 * k4 + j) for j in range(4)])

    def store_xt(self):
        for k in range(16):
            o = self.dma("sp", self.xt_out[:, k, :], self.XT[:, k, :], [("XT", k)], [("xt_out", k)])
            self.finals.append(o)

    def store_y(self):
        for tb in range(8):
            sl = tb % 2
            sv, sres = self.stage(sl)
            for k4 in range(4):
                b = self.nb()
                for j in range(4):
                    k = 4 * k4 + j
                    self.tr(self.ps[b][:, j * 128:(j + 1) * 128], self.XT[:, k, tb * 128:(tb + 1) * 128], [("XT", k)], [("ps", b)])
                self.copy(self.ev_eng(), sv[:, k4 // 2, (k4 % 2) * 512:(k4 % 2 + 1) * 512], self.ps[b][:, :], [("ps", b)], sres)
            o = self.dma("sp", self.y_out[tb * 128:(tb + 1) * 128, :].rearrange("p (a t) -> p a t", a=2), sv, sres, [("y", tb)],
                         key=("yst", sl))
            self.finals.append(o)

    def norm(self, l, a):
        for k in range(16):
            self.act(self.HT[:, k, :], self.XT[:, k, :], AF.Square, [("XT", k)], [("HT", k)])
        b = [self.nb(), self.nb()]
        for k in range(16):
            for half in range(2):
                self.mm(self.ps[b[half]][:, :], self.ONESB[:, :], self.HT[:, k, half * 512:(half + 1) * 512], k == 0, k == 15,
                        reads=["ONESB", ("HT", k)], writes=[("ps", b[half])])
        for half in range(2):
            sl = slice(half * 512, (half + 1) * 512)
            self.act(self.RS[:, sl], self.ps[b[half]][:, :], AF.Sqrt, [("ps", b[half]), "EPSC"], [("S", 3)],
                     bias=self.EPSC[:, 0:1], scale=1.0 / D)
            self.recip(self.RS[:, sl], self.RS[:, sl], [("S", 3)], [("S", 3)])
        for k in range(16):
            for half in range(2):
                sl = slice(half * 512, (half + 1) * 512)
                self.stt(self.HT[:, k, sl], self.XT[:, k, sl], self.GN[:, l, a, k:k + 1], self.RS[:, sl], ALU.mult, ALU.mult,
                         [("XT", k), "GN", ("S", 3)], [("HT", k)])

    def headnorm(self, b, gain, out, out_res):
        for half in range(2):
            self.act(self.SQ[:, half * 512:(half + 1) * 512], self.ps[b[half]][:, :], AF.Square, [("ps", b[half])], [("SQ", half)])
        b2 = [self.nb(), self.nb()]
        for half in range(2):
            self.mm(self.ps[b2[half]][:, :], self.BLK[:, :], self.SQ[:, half * 512:(half + 1) * 512], True, True,
                    reads=["BLK", ("SQ", half)], writes=[("ps", b2[half])])
        for half in range(2):
            sl = slice(half * 512, (half + 1) * 512)
            self.act(self.RS[:, sl], self.ps[b2[half]][:, :], AF.Sqrt, [("ps", b2[half]), "EPSC"], [("S", 3)],
                     bias=self.EPSC[:, 0:1], scale=1.0 / 64)
            self.recip(self.RS[:, sl], self.RS[:, sl], [("S", 3)], [("S", 3)])
            self.stt(out[:, sl], self.ps[b[half]][:, :], gain, self.RS[:, sl], ALU.mult, ALU.mult,
                     [("ps", b[half]), ("S", 3), "QKG", "GQ8"], out_res)

    def kv(self, l):
        for t in range(2):
            s = self.get_tile()
            wvw = self.wv(s, 16, 384)
            for j in range(3):
                c = 3 * t + j
                b = self.fm(wvw, j, 16, self.ht_rhs, self.ht_res, ("WS", s))
                kst = c % 2
                self.headnorm(b, self.QKG[:, l, 1:2], self.KST[kst], [("BS", kst)])
                self.dma("sp", self.kx[l][c * 128:(c + 1) * 128, :], self.KST[kst], [("BS", kst)], [("kx", l)], key=("kx", l, kst))
        for t in range(2):
            s = self.get_tile()
            if t == 0 and self.mode in ("F", "G"):
                self.exchange_kv(l, 0)
                self.exchange_kv(l, 1)
            wvw = self.wv(s, 16, 384)
            vres = [("YB", 0, c) for c in range(3)]
            vst = self.YB[0][:, 0:3, :].rearrange("p c t -> p (c t)").rearrange("p (n f) -> p n f", f=384)
            for tb in range(8):
                b = self.nb()
                for k in range(16):
                    self.mm(self.ps[b][:, 0:384], self.HT[:, k, tb * 128:(tb + 1) * 128], wvw[:, k, :], k == 0, k == 15,
                            reads=[("WS", s), ("HT", k)], writes=[("ps", b)])
                self.copy(self.ev_eng(), vst[:, tb, :], self.ps[b][:, 0:384], [("ps", b)], vres)
            self.dma("sp", self.vx[l][:, 384 * t:384 * t + 384].rearrange("(tb p) f -> p tb f", p=128), vst, vres, [("vx", l)],
                     key=("vx", l), grp=True)

    def bmix(self, l):
        ZR = ("S", 2)
        self.memset(self.Z[:, 0:2], 0.0, [ZR])
        for c in range(6):
            s = self.get_tile()
            if c == 2 and self.mode in ("F", "G"):
                self.exchange_kv(l, 2)
                self.exchange_kv(l, 3)
            wvw = self.wv(s, 16, 384)
            bb = self.fm(wvw, 0, 16, self.ht_rhs, self.ht_res, ("WS", s))
            bc = self.fm(wvw, 1, 16, self.ht_rhs, self.ht_res, ("WS", s))
            bx = self.fm(wvw, 2, 16, self.ht_rhs, self.ht_res, ("WS", s))
            for half in range(2):
                sl = slice(half * 512, (half + 1) * 512)
                zsl = slice(2 + half * 512, 2 + (half + 1) * 512)
                self.act(self.S[:, 0, sl], self.ps[bc[half]][:, :], AF.Copy, [("ps", bc[half])], [("S", 0)])
                self.tt(self.Z[:, zsl], self.S[:, 0, sl], self.ps[bx[half]][:, :], ALU.mult, [("S", 0), ("ps", bx[half])], [ZR])
            self.copy("dve", self.ZL[:, c, :], self.Z[:, T:T + 2], [ZR], ["ZL"])
            self.act(self.S[:, 1, 0:T], self.Z[:, 2:T + 2], AF.Copy, [ZR, "CW"], [("S", 1)], scale=self.CW[:, l, c, 2:3])
            self.stt(self.S[:, 1, 0:T], self.Z[:, 1:T + 1], self.CW[:, l, c, 1:2], self.S[:, 1, 0:T], ALU.mult, ALU.add,
                     [ZR, "CW", ("S", 1)], [("S", 1)])
            self.stt(self.S[:, 1, 0:T], self.Z[:, 0:T], self.CW[:, l, c, 0:1], self.S[:, 1, 0:T], ALU.mult, ALU.add,
                     [ZR, "CW", ("S", 1)], [("S", 1)])
            self.copy("dve", self.ACC01[:, c, :], self.S[:, 1, 0:2], [("S", 1)], ["ACC01"])
            self.copy("dve", self.B01[:, c, :], self.ps[bb[0]][:, 0:2], [("ps", bb[0])], ["B01"])
            for half in range(2):
                sl = slice(half * 512, (half + 1) * 512)
                self.tt(self.YB[1][:, c, sl], self.S[:, 1, sl], self.ps[bb[half]][:, :], ALU.mult, [("S", 1), ("ps", bb[half])],
                        [("YB", 1, c)])
        self.dma("sp", self.zx[l].rearrange("(c p) t -> p c t", p=128), self.ZL[:, :, :], ["ZL"], [("zx", l)])

    def fixup(self, l):
        self.load_zh(l)
        fx = self.FX
        zh, cw, a01, b01 = self.ZH, self.CW, self.ACC01, self.B01
        R = ["ZH", "CW", "ACC01", "B01", "FX"]
        self.tt(fx[:, 0, :], zh[:, :, 0], cw[:, l, :, 0], ALU.mult, R, ["FX"])
        self.tt(fx[:, 0, :], fx[:, 0, :], a01[:, :, 0], ALU.add, R, ["FX"])
        self.tt(fx[:, 1, :], zh[:, :, 1], cw[:, l, :, 1], ALU.mult, R, ["FX"])
        self.tt(fx[:, 0, :], fx[:, 0, :], fx[:, 1, :], ALU.add, R, ["FX"])
        self.tt(fx[:, 2, :], zh[:, :, 1], cw[:, l, :, 0], ALU.mult, R, ["FX"])
        self.tt(fx[:, 2, :], fx[:, 2, :], a01[:, :, 1], ALU.add, R, ["FX"])
        yres = [("YB", 1, c) for c in range(6)]
        self.tt(self.YB[1][:, :, 0], fx[:, 0, :], b01[:, :, 0], ALU.mult, R, yres)
        self.tt(self.YB[1][:, :, 1], fx[:, 2, :], b01[:, :, 1], ALU.mult, R, yres)

    def wout(self, l, yb, c0, nr):
        s = self.get_tile()
        wvw = self.wv(s, nr, 2048)
        for dch in range(16):
            b = self.fm(wvw, dch, nr, lambda k, half: self.YB[yb][:, c0 + k, half * 512:(half + 1) * 512],
                        lambda k: [("YB", yb, c0 + k)], ("WS", s))
            for half in range(2):
                sl = slice(half * 512, (half + 1) * 512)
                self.tt(self.XT[:, dch, sl], self.XT[:, dch, sl], self.ps[b[half]][:, :], ALU.add,
                        [("XT", dch), ("ps", b[half])], [("XT", dch)])

    def amix(self, l):
        self.dma("sp", self.SGWT, self.sgwt_in[:, l, :, :], [], [("S", 0)])
        self.dma("sp", self.SGB, self.sgb_in[:, l, :, :], [], [("S", 1)])
        for h_ in range(8):
            self.tt(self.WCT[:, h_, :], self.SGWT[:, h_, :], self.CF[:, 128:256], ALU.mult, [("S", 0), "CF"], ["WCT"])
        VAR = [("BS", 2), ("BS", 3)]
        for t in range(2):
            s = self.get_tile()
            wvw = self.wv(s, 16, 512)
            for tb in range(8):
                b = self.nb()
                for k in range(16):
                    self.mm(self.ps[b][:, 0:256], self.HT[:, k, tb * 128:(tb + 1) * 128], wvw[:, k, 256:512], k == 0, k == 15,
                            reads=[("WS", s), ("HT", k)], writes=[("ps", b)])
                self.copy(self.ev_eng(), self.VA[:, tb, :], self.ps[b][:, 0:256], [("ps", b)], VAR)
            for cc in range(2):
                c = 2 * t + cc
                bu = self.fm(wvw, cc, 16, self.ht_rhs, self.ht_res, ("WS", s))
                for half in range(2):
                    self.act(self.S[:, 2, half * 512:(half + 1) * 512], self.ps[bu[half]][:, :], AF.Copy, [("ps", bu[half])], [("S", 2)])
                bm = [self.nb(), self.nb()]
                for tb in range(8):
                    half, off = tb // 4, (tb % 4) * 128
                    for e in range(2):
                        self.mm(self.ps[bm[half]][64 * e:64 * e + 64, off:off + 128],
                                self.VA[:, tb, cc * 128 + 64 * e:cc * 128 + 64 * e + 64], self.WCT[:, 2 * c + e, :], True, False,
                                reads=VAR + ["WCT"], writes=[("ps", bm[half])], tp=(0, 64 * e))
                    self.mm(self.ps[bm[half]][:, off:off + 128], self.SEL2[:, :], self.SGB[:, c, :], False, True,
                            reads=["SEL2", ("S", 1)], writes=[("ps", bm[half])])
                for half in range(2):
                    sl = slice(half * 512, (half + 1) * 512)
                    self.tt(self.YB[0][:, c, sl], self.S[:, 2, sl], self.ps[bm[half]][:, :], ALU.mult, [("S", 2), ("ps", bm[half])],
                            [("YB", 0, c)])

    def load_zh(self, l):
        if self.mode == "F":
            self.dma("sp", self.ZH[:], self.zwl[l][1].rearrange("(c p) t -> p c t", p=128), [("zwl", l)], ["ZH"])
        else:
            self.dma("sp", self.ZH[:], self.zwin[l].rearrange("(c p) t -> p c t", p=128), [], ["ZH"])

    def load_kwin(self, l, g, kc_, buf):
        d, hp = PATS[g]
        rows = slice(kc_ * 128, (kc_ + 1) * 128)
        KW = self.KWb[buf]
        res = [("HT", 3 * buf + i) for i in range(3)]
        F = self.mode == "F"
        rd = [("kwl", l)] if F else []
        kw = self.kwl[l] if F else self.kwin[l]
        key = ("KW", buf)
        if hp == 2048:
            self.dma("sp", KW[:, 0:T], kw[0, rows, :], rd, res, key=key, grp=True)
            self.dma("sp", KW[:, T:2 * T], kw[1, rows, :], rd, res, key=key, grp=True)
        else:
            self.dma("sp", KW[:, 0:hp], kw[1, rows, T - hp:T], rd, res, key=key, grp=True)
        self.dma("sp", KW[:, hp:hp + T], kw[2, rows, :], rd, res, key=key, grp=True)

    def load_vwin(self, l, g, kc_, buf):
        d, hp = PATS[g]
        L = T // d
        nk = 128 + L
        F = self.mode == "F"
        rd = [("vwl", l)] if F else []
        vw = self.vwl[l] if F else self.vwin[l]
        VW = self.VWb[buf]
        res = [("HT", 6 + 4 * buf + i) for i in range(4)]
        jb = 0
        while jb * 128 < nk:
            nj = min(128, nk - jb * 128)
            w0 = (2048 - hp) + d * 128 * jb
            src = vw[w0:w0 + d * nj, kc_ * 128:(kc_ + 1) * 128].rearrange("(j r) f -> j r f", r=d)
            self.dma("sp", VW[0:nj, jb * d:(jb + 1) * d, :], src, rd, res, key=("VW", buf), grp=True)
            jb += 1

    def attn_prep(self, l):
        tn = self.TINY
        QR = ("S", 3)
        self.dma("sp", self.QKROW, self.qkrow_in[:, l, :], [], [QR])
        R = [QR, "TINY"]
        self.tt(self.QKROW, self.QKROW, self.QKROW, ALU.mult, R, [QR])
        self.P.add("dve", lambda h: h.tensor_reduce(out=tn[:, 0:2], in_=self.QKROW.rearrange("p (a d) -> p a d", a=2),
                                                    axis=AX.X, op=ALU.max), reads=R, writes=["TINY"])
        self.tt(tn[:, 2:3], tn[:, 0:1], tn[:, 1:2], ALU.mult, R, ["TINY"])
        self.act(tn[:, 3:4], tn[:, 2:3], AF.Sqrt, R, ["TINY"], scale=64.0)
        self.P.add("dve", lambda h: h.tensor_scalar(out=tn[:, 4:5], in0=tn[:, 3:4], scalar1=-1.0, scalar2=0.0, op0=ALU.mult, op1=ALU.add),
                   reads=R, writes=["TINY"])
        b = self.nb()
        self.mm(self.ps[b][:, 0:1], self.ONEF[:, :], tn[:, 4:5], True, True, reads=["ONEF", "TINY"], writes=[("ps", b)])
        self.copy("dve", self.NEGC[:, :], self.ps[b][:, 0:1], [("ps", b)], ["NEGC"])

    def attention(self, l):
        self.attn_prep(l)
        self.KWb = [self.HT[:, 3 * i:3 * i + 3, :].rearrange("p c t -> p (c t)") for i in range(2)]
        self.VWb = [self.HT[:, 6 + 4 * i:10 + 4 * i, :].rearrange("p a t -> p (a t)").rearrange("p (n f) -> p n f", f=128) for i in range(2)]
        for pr in range(2):
            s = self.get_tile()
            wvw = self.wv(s, 16, 384)
            for g in range(3):
                b = self.fm(wvw, g, 16, self.ht_rhs, self.ht_res, ("WS", s))
                self.headnorm(b, self.GQ8[:, l:l + 1], self.YB[1][:, 3 * pr + g, :], [("YB", 1, 3 * pr + g)])
        st = dict(etc=0, scq=0, accq=0)
        groups = [(pr, g) for pr in range(2) for g in range(3)]
        items = []

        def mk_load(gi):
            pr, g = groups[gi]
            def back():
                self.load_kwin(l, g, 2 * g + pr, gi % 2)
                self.load_vwin(l, g, 2 * g + pr, gi % 2)
            return dict(front=None, back=back)

        def mk_final(pr):
            def back():
                self.recip(self.S[:, 3, 0:T], self.S[:, 3, 0:T], [("S", 3)], [("S", 3)])
                for g in range(3):
                    c = 2 * g + pr
                    self.tt(self.YB[0][:, c, :], self.S[:, g, 0:T], self.S[:, 3, 0:T], ALU.mult, [("S", g), ("S", 3)], [("YB", 0, c)])
            return dict(front=None, back=back)

        def mk_tile(gi, r, qb0, qb1, bn, bd, e, jb, last):
            pr, g = groups[gi]
            d, hp = PATS[g]
            L = T // d
            qn = min(L, 128)
            nqb = max(1, L // 128)
            qps = nqb // (2 if nqb == 8 else 1)
            buf = gi % 2
            KW, VW = self.KWb[buf], self.VWb[buf]
            KWR = [("HT", 3 * buf + i) for i in range(3)]
            VWR = [("HT", 6 + 4 * buf + i) for i in range(4)]
            qi = 3 * pr + g
            tl = {}

            def front():
                rows = slice(64 * e, 64 * e + 64)
                nj = min(128, 128 + L - jb * 128)
                parts = []
                if qb0 <= jb - 1 <= qb1:
                    parts.append((jb - 1, "hi"))
                if qb0 <= jb <= qb1:
                    parts.append((jb, "lo"))
                nq = qn * len(parts)
                q0 = parts[0][0] * qn
                kcol = slice(d * 128 * jb + r, d * (128 * jb + nj - 1) + r + 1, d)
                qcol = slice(d * q0 + r, d * (q0 + nq - 1) + r + 1, d)
                bs = 4 + st["scq"] % 4
                st["scq"] += 1
                self.mm(self.ps[bs][0:nj, 0:nq], KW[rows, kcol], self.YB[1][rows, qi, qcol], True, True,
                        reads=KWR + [("YB", 1, qi)], writes=[("ps", bs)])
                et = self.ET[st["etc"] % 4]
                eres = [("ET", st["etc"] % 4)]
                st["etc"] += 1
                self.act(et[0:nj, 0:nq], self.ps[bs][0:nj, 0:nq], AF.Exp, [("ps", bs), "NEGC"], eres,
                         bias=self.NEGC[0:nj, 0:1], scale=1.0)
                if jb == 0:
                    mk = self.CB[0:nj, 256 + 128 * g:256 + 128 * g + nq]
                elif len(parts) == 2:
                    mk = self.CB[0:nj, 0:256]
                elif parts[0][1] == "hi":
                    mk = self.CB[0:nj, 0:nq]
                else:
                    mk = self.CB[0:nj, 128:128 + nq]
                self.tt(et[0:nj, 0:nq], et[0:nj, 0:nq], mk, ALU.mult, eres + ["CB"], eres)
                tl.update(nj=nj, parts=parts, et=et, eres=eres, rows=rows)

            def back():
                nj, et, eres, rows = tl["nj"], tl["et"], tl["eres"], tl["rows"]
                for pi, (qb, kind) in enumerate(tl["parts"]):
                    oc = slice((qb - qb0) * qn, (qb - qb0 + 1) * qn)
                    first, lastp = (kind == "lo"), (kind == "hi")
                    rhs = et[0:nj, pi * qn:(pi + 1) * qn]
                    self.mm(self.ps[bn][rows, oc], VW[0:nj, jb * d + r, 64 * e:64 * e + 64], rhs, first, lastp,
                            reads=VWR + eres, writes=[("ps", bn)], tp=(0, 64 * e))
                    self.mm(self.ps[bd][rows, oc], self.ONESB[0:nj, 0:64], rhs, first, lastp,
                            reads=["ONESB"] + eres, writes=[("ps", bd)], tp=(0, 64 * e))
                if last:
                    ntok = qps * qn
                    i0 = qb0 * qn
                    tcol = slice(d * i0 + r, d * (i0 + ntok - 1) + r + 1, d)
                    self.act(self.S[:, g, tcol], self.ps[bn][:, 0:ntok], AF.Copy, [("ps", bn)], [("S", g)])
                    if g == 0:
                        self.copy("dve", self.S[:, 3, tcol], self.ps[bd][:, 0:ntok], [("ps", bd)], [("S", 3)])
                    else:
                        self.tt(self.S[:, 3, tcol], self.S[:, 3, tcol], self.ps[bd][:, 0:ntok], ALU.add, [("S", 3), ("ps", bd)],
                                [("S", 3)])
            return dict(front=front, back=back)

        mk_load(0)["back"]()
        mk_load(1)["back"]()
        for gi, (pr, g) in enumerate(groups):
            d, hp = PATS[g]
            L = T // d
            nqb = max(1, L // 128)
            nsup = 2 if nqb == 8 else 1
            qps = nqb // nsup
            if gi >= 1 and gi + 1 < len(groups):
                items.append(mk_load(gi + 1))
            for r in range(d):
                for sp_ in range(nsup):
                    qb0, qb1 = sp_ * qps, sp_ * qps + qps - 1
                    bn, bd = (0, 1) if st["accq"] % 2 == 0 else (2, 3)
                    st["accq"] += 1
                    gt = [(e, jb) for e in range(2) for jb in range(qb0, qb1 + 2)]
                    for ti, (e, jb) in enumerate(gt):
                        items.append(mk_tile(gi, r, qb0, qb1, bn, bd, e, jb, ti == len(gt) - 1))
            if g == 2:
                items.append(mk_final(pr))
        LA = 3
        for i in range(len(items) + LA):
            if i < len(items) and items[i]["front"] is not None:
                items[i]["front"]()
            if i >= LA:
                items[i - LA]["back"]()

    def mlp(self, l):
        self.norm(l, 1)

        def w1stage(g):
            s = self.get_tile()
            wvw = self.wv(s, 16, 512)
            hb = g % 2
            for f in range(4):
                b = self.fm(wvw, f, 16, self.ht_rhs, self.ht_res, ("WS", s))
                for half in range(2):
                    rt = self.RT[half]
                    self.act(rt, self.ps[b[half]][:, :], AF.Relu, [("ps", b[half])], [("S", half)])
                    self.act(self.YB[hb][:, f, half * 512:(half + 1) * 512], rt, AF.Square, [("S", half)], [("YB", hb, f)])

        def w2stage(g):
            self.wout(l, g % 2, 0, 4)

        w1stage(0)
        for g in range(16):
            if g + 1 < 16:
                w1stage(g + 1)
            w2stage(g)

    def zero_pads(self):
        zres = [("YB", 0, c) for c in range(6)]
        self.memset(self.YB[0][:, :, :], 0.0, zres)
        self.memset(self.FX[:, :, :], 0.0, ["FX"])
        zsrc = self.YB[0][:, :, :].rearrange("p c t -> p (c t)")
        for l in self.layers:
            for h_ in range(2):
                self.dma("sp", self.kwin[l][h_][0:2].rearrange("b (p a) t -> p b (a t)", p=128),
                         zsrc.rearrange("p (b x) -> p b x", b=2), zres, [("kgpad", l, h_)])
                self.dma("sp", self.vwin[l][h_][0:2].rearrange("b (p a) f -> p b (a f)", p=128),
                         zsrc.rearrange("p (b x) -> p b x", b=2), zres, [("vgpad", l, h_)])
            for bi in range(2):
                self.dma("sp", self.zwin[l][bi].rearrange("(p a) t -> p (a t)", p=128),
                         self.FX[:, :, :].rearrange("p a b -> p (a b)")[:, 0:12], ["FX"], [("zgpad", l, bi)])

        def setrv(h):
            r = h.partition_id() % 4
            self.rvb = [h.snap(r, min_val=0, max_val=3)]
            return None
        self.P.add("sp", setrv, reads=[], writes=["rv"], nosync=True)

    def _cc(self, src, dst, rres, wres, key):
        groups = [[0, 1, 2, 3], [4, 5, 6, 7]]
        self.P.add("pool", lambda h: h.collective_compute("AllGather", ALU.bypass, replica_groups=groups, ins=[src], outs=[dst]),
                   reads=rres, writes=wres, dma=True, key=key, inc=self.cc_inc)

    def exchange_kv(self, l, i):
        h_ = i % 2
        if i < 2:
            self._cc(self.kx[l][384 * h_:384 * h_ + 384, :], self.kwin[l][h_][2:6].rearrange("b f t -> (b f) t"),
                     [("kx", l), ("kgpad", l, h_)], [("kg", l, h_)], ("cck", l, h_))
        else:
            self._cc(self.vx[l][512 * h_:512 * h_ + 512, :], self.vwin[l][h_][2:6].rearrange("b t f -> (b t) f"),
                     [("vx", l), ("vgpad", l, h_)], [("vg", l, h_)], ("ccv", l, h_))

    def exchange_z(self, l):
        self._cc(self.zx[l], self.zwin[l][2:6].rearrange("b f t -> (b f) t"), [("zx", l), ("zgpad", l, 0), ("zgpad", l, 1)],
                 [("zg", l)], ("ccz", l))

    def localize(self, l):
        v3 = self.vwl[l].rearrange("(b t) f -> b t f", b=3)
        for h_ in range(2):
            self.dma("sp", self.kwl[l][:, 384 * h_:384 * h_ + 384, :],
                     (lambda h_=h_: self.kwin[l][h_][bass.ds(self.rvb[0], 3), :, :]), [("kg", l, h_), "rv"], [("kwl", l)], key=("kwl", l), grp=True)
            self.dma("sp", v3[:, 512 * h_:512 * h_ + 512, :],
                     (lambda h_=h_: self.vwin[l][h_][bass.ds(self.rvb[0], 3), :, :]), [("vg", l, h_), "rv"], [("vwl", l)], key=("vwl", l), grp=True)
        self.dma("sp", self.zwl[l].rearrange("b f t -> (b f) t"),
                 lambda: self.zwin[l][bass.ds(self.rvb[0], 3), :, :].rearrange("b f t -> (b f) t"), [("zg", l), "rv"], [("zwl", l)])

    def pre(self, l, do_kv=True):
        self.norm(l, 0)
        if do_kv:
            self.kv(l)
        else:
            for _ in range(4):
                self.get_tile()
        self.bmix(l)
        if self.mode in ("F", "G"):
            self.exchange_z(l)

    def post(self, l):
        self.amix(l)
        self.wout(l, 0, 0, 4)
        if self.mode == "F":
            self.localize(l)
        self.fixup(l)
        self.wout(l, 1, 0, 3)
        self.wout(l, 1, 3, 3)
        self.attention(l)
        self.wout(l, 0, 0, 3)
        self.wout(l, 0, 3, 3)
        self.mlp(l)

    def build(self, cc_inc=1):
        m = self.mode
        self.cc_inc = cc_inc
        self.plan_tiles()
        self.consts()
        self.load_x()
        if m == "F":
            self.zero_pads()
        if m == "A":
            self.pre(0)
            for key in (("kx", 0), ("vx", 0), ("zx", 0)):
                self.finals.append(self.P.last_w[key])
        elif m == "B":
            self.pre(0)
            self.post(0)
            self.pre(1)
            for key in (("kx", 1), ("vx", 1), ("zx", 1)):
                self.finals.append(self.P.last_w[key])
            self.store_xt()
        elif m == "C":
            self.pre(1)
            self.post(1)
            self.store_y()
        elif m == "G":
            self.zero_pads()
            self.pre(0)
            self.localize(0)
            self.gk = self.dram_out("gk", [3, 768, T], BF16)
            self.gv = self.dram_out("gv", [3 * T, 768], BF16)
            self.gz = self.dram_out("gz", [3, 768, 2], F32)
            self.finals.append(self.dma("sp", self.gk.rearrange("b f t -> (b f) t"), self.kwl[0].rearrange("b f t -> (b f) t"), [("kwl", 0)], ["gk"]))
            self.finals.append(self.dma("sp", self.gv, self.vwl[0], [("vwl", 0)], ["gv"]))
            self.finals.append(self.dma("sp", self.gz.rearrange("b f t -> (b f) t"), self.zwl[0].rearrange("b f t -> (b f) t"), [("zwl", 0)], ["gz"]))
        else:
            for l in (0, 1):
                self.pre(l)
                self.post(l)
            self.store_y()
        with self.nc.allow_non_contiguous_dma(reason="tiny halo / param layouts"):
            fin = list(self.finals)
            for e in ENGS:
                if self.P.q[e]:
                    fin.append(self.P.q[e][-1])
            self.P.emit(final_waits=fin)
        self.st.close()
        return self.nc


def _consts(rank):
    jj = np.arange(128)[:, None]
    ii = np.arange(128)[None, :]
    m_hi = (jj <= ii).astype(np.float32)
    m_lo = (jj >= ii).astype(np.float32)
    cf = np.concatenate([np.eye(128, dtype=np.float32), m_hi], axis=1)
    hm = []
    for g in range(3):
        if g < 2:
            valid = np.full((128, 1), 1.0 if rank >= 1 else 0.0, np.float32)
        else:
            valid = np.zeros((128, 1), np.float32)
            if rank >= 2:
                valid[:] = 1.0
            elif rank == 1:
                valid[64:] = 1.0
        hm.append(m_lo * valid)
    cb = np.concatenate([m_hi, m_lo] + hm + [np.eye(128, dtype=np.float32)], axis=1).astype(NPBF)
    sel2 = np.zeros((2, 128), np.float32)
    sel2[0, :64] = 1.0
    sel2[1, 64:] = 1.0
    return cf, cb, sel2


def _params(inp):
    gn = np.stack([inp["attn_norm"], inp["mlp_norm"]], axis=1)
    gn = np.ascontiguousarray(gn.reshape(2, 2, 16, 128).transpose(3, 0, 1, 2))
    cw = np.ascontiguousarray(inp["conv_w"].reshape(2, 3, 6, 128).transpose(3, 0, 2, 1))
    qk = np.stack([inp["q_norm"], inp["k_norm"]], axis=2)
    qkg = np.ascontiguousarray(np.concatenate([qk, qk], axis=1).transpose(1, 0, 2))
    qkrow = np.ascontiguousarray(np.concatenate([inp["q_norm"], inp["k_norm"]], axis=1)[None])
    sgwt = np.ascontiguousarray(inp["sgu_w"].transpose(3, 0, 1, 2))
    sgb = np.ascontiguousarray(inp["sgu_b"].reshape(2, 4, 2, 128).transpose(2, 0, 1, 3))
    return dict(gn=gn, cw=cw, qkg=qkg, qkrow=qkrow, sgwt=sgwt, sgb=sgb)


_CACHE = {}


def _get_nc(mode, **kw):
    key = (mode, tuple(sorted(kw.items())))
    if key not in _CACHE:
        _CACHE[key] = Builder(mode).build(**kw)
    return _CACHE[key]


def _windows(kxs, vxs, zxs):
    outs = []
    for c in range(8):
        r = c % 4
        kw = np.zeros((3, 768, T), NPBF)
        vw = np.zeros((3 * T, 768), NPBF)
        zw = np.zeros((768, 2), np.float32)
        for bi in range(3):
            src = r - 2 + bi
            if src >= 0:
                kw[bi] = kxs[c - r + src]
                vw[bi * T:(bi + 1) * T] = vxs[c - r + src]
        if r >= 1:
            zw = zxs[c - 1]
        outs.append((kw, vw, zw))
    return outs


FUSED = True


def kernel(**inp):
    inp = {k: np.asarray(v) for k, v in inp.items()}
    x = np.ascontiguousarray(inp["x"].reshape(8, T, D))
    prm = _params(inp)
    base = []
    for c in range(8):
        cf, cb, sel2 = _consts(c % 4)
        d = dict(prm)
        d.update(cf=cf, cb=cb, sel2=sel2)
        base.append(d)
    W = {}
    for l in range(2):
        W["w_in%d" % l] = np.ascontiguousarray(inp["w_in"][l])
        W["w_out%d" % l] = np.ascontiguousarray(inp["w_out"][l])
        W["w1_%d" % l] = np.ascontiguousarray(inp["w_mlp_in"][l])
        W["w2_%d" % l] = np.ascontiguousarray(inp["w_mlp_out"][l])
    cores = list(range(8))
    if FUSED:
        ncF = _get_nc("F")
        mapsF = [dict(base[c], x=x[c], **W) for c in cores]
        rF = run_bass_kernel_spmd(ncF, mapsF, core_ids=cores).results
        y = np.stack([rF[c]["y"] for c in cores], axis=0).reshape(2, 4 * T, D)
        return y.astype(np.float32)
    return _kernel_unfused(x, base, W)


def _kernel_unfused(x, base, W):
    cores = list(range(8))
    ncA = _get_nc("A")
    mapsA = [dict(base[c], x=x[c], w_in0=W["w_in0"]) for c in cores]
    rA = run_bass_kernel_spmd(ncA, mapsA, core_ids=cores).results
    win0 = _windows([r["kx0"] for r in rA], [r["vx0"] for r in rA], [r["zx0"] for r in rA])
    ncB = _get_nc("B")
    mapsB = [dict(base[c], x=x[c], w_in0=W["w_in0"], w_out0=W["w_out0"], w1_0=W["w1_0"], w2_0=W["w2_0"], w_in1=W["w_in1"],
                  kwin0=win0[c][0], vwin0=win0[c][1], zwin0=win0[c][2]) for c in cores]
    rB = run_bass_kernel_spmd(ncB, mapsB, core_ids=cores).results
    win1 = _windows([r["kx1"] for r in rB], [r["vx1"] for r in rB], [r["zx1"] for r in rB])
    ncC = _get_nc("C")
    mapsC = [dict(base[c], xt_in=rB[c]["xt_out"], w_in1=W["w_in1"], w_out1=W["w_out1"], w1_1=W["w1_1"], w2_1=W["w2_1"],
                  kwin1=win1[c][0], vwin1=win1[c][1], zwin1=win1[c][2]) for c in cores]
    rC = run_bass_kernel_spmd(ncC, mapsC, core_ids=cores).results
    y = np.stack([rC[c]["y"] for c in cores], axis=0).reshape(2, 4 * T, D)
    return y.astype(np.float32)
```

```python
import contextlib
import numpy as np
import ml_dtypes
import concourse.bass as bass
import concourse.mybir as mybir
from concourse.bass_utils import run_bass_kernel_spmd

F32 = mybir.dt.float32
BF16 = mybir.dt.bfloat16
I32 = mybir.dt.int32
AF = mybir.ActivationFunctionType
ALU = mybir.AluOpType
AX = mybir.AxisListType
NPBF = ml_dtypes.bfloat16

ENGS = ("pe", "act", "dve", "pool", "sp")
T = 1024
D = 2048
DIN = 5632
DFF = 8192
EPS = 1e-6
PATS = ((1, 128), (4, 512), (16, 2048))


class Op:
    __slots__ = ("eng", "fn", "deps", "dma", "key", "ms", "has_cons", "gidx", "inc", "nosync")

    def __init__(self, eng, fn, dma, key, gidx, inc):
        self.eng = eng
        self.fn = fn
        self.deps = []
        self.dma = dma
        self.key = key
        self.ms = None
        self.has_cons = False
        self.gidx = gidx
        self.inc = inc
        self.nosync = False


class Prog:
    def __init__(self, nc, same_engine_sync=True):
        self.nc = nc
        self.q = {e: [] for e in ENGS}
        self.last_w = {}
        self.readers = {}
        self.n = 0
        self.same_engine_sync = same_engine_sync

    def add(self, eng, fn, reads=(), writes=(), dma=False, key=None, inc=None, grp=False, nosync=False):
        if dma and key is None:
            key = writes[0]
        op = Op(eng, fn, dma, key, self.n, inc if inc is not None else (16 if dma else 1))
        op.nosync = nosync
        self.n += 1
        deps = {}
        for r in reads:
            w = self.last_w.get(r)
            if w is not None:
                deps[id(w)] = w
        for w_ in writes:
            w = self.last_w.get(w_)
            if w is not None and not (grp and w.dma and w.key == key):
                deps[id(w)] = w
            for rd in self.readers.get(w_, ()):
                deps[id(rd)] = rd
        op.deps = list(deps.values())
        for d in op.deps:
            d.has_cons = True
        for r in reads:
            self.readers.setdefault(r, []).append(op)
        for w_ in writes:
            self.last_w[w_] = op
            self.readers[w_] = []
        self.q[eng].append(op)
        return op

    def emit(self, final_waits=()):
        nc = self.nc
        eng_cnt = {e: 0 for e in ENGS}
        key_cnt = {}
        allops = sorted([o for e in ENGS for o in self.q[e]], key=lambda o: o.gidx)
        for o in final_waits:
            o.has_cons = True
        for o in allops:
            if o.dma:
                key_cnt[o.key] = key_cnt.get(o.key, 0) + o.inc
                o.ms = key_cnt[o.key]
            elif o.has_cons and not o.nosync:
                eng_cnt[o.eng] += 1
                o.ms = eng_cnt[o.eng]
        keys = sorted(key_cnt.keys(), key=str)
        with contextlib.ExitStack() as st:
            esem = {e: st.enter_context(nc.semaphore("s_" + e)) for e in ENGS}
            ksem = {k: st.enter_context(nc.semaphore("k%d" % i)) for i, k in enumerate(keys)}
            block = st.enter_context(nc.Block())
            self.nsem = len(esem) + len(ksem)

            def run(ename, h):
                waited = {}
                for o in self.q[ename]:
                    for d in o.deps:
                        if d.nosync:
                            continue
                        if d.dma:
                            s, v = ksem[d.key], d.ms
                        else:
                            if d.eng == ename and (ename == "pe" or not self.same_engine_sync):
                                continue
                            s, v = esem[d.eng], d.ms
                        if waited.get(id(s), 0) >= v:
                            continue
                        waited[id(s)] = v
                        h.wait_ge(s, v)
                    ins = o.fn(h)
                    if o.dma:
                        ins.then_inc(ksem[o.key], o.inc)
                    elif o.has_cons and not o.nosync:
                        ins.then_inc(esem[o.eng], 1)
                if ename == "sp":
                    for d in final_waits:
                        if d.nosync:
                            continue
                        s, v = (ksem[d.key], d.ms) if d.dma else (esem[d.eng], d.ms)
                        h.wait_ge(s, v)

            @block.tensor
            def _(h):
                run("pe", h)

            @block.scalar
            def _(h):
                run("act", h)

            @block.vector
            def _(h):
                run("dve", h)

            @block.gpsimd
            def _(h):
                run("pool", h)

            @block.sync
            def _(h):
                run("sp", h)


class Builder:
    def __init__(self, mode):
        self.mode = mode
        nc = self.nc = bass.Bass("TRN2", target_bir_lowering=False)
        self.st = contextlib.ExitStack()
        self.P = Prog(nc)
        self.bank = 0
        self.wcnt = 0
        self.wissued = 0
        self.wplan = []
        self.finals = []
        self.evq = 0
        self.layers = {"A": [0], "B": [0, 1], "C": [1], "F": [0, 1], "G": [0]}[mode]
        self._declare()

    def dram_in(self, name, shape, dt):
        return self.nc.dram_tensor(name, list(shape), dt, kind="ExternalInput").ap()

    def dram_out(self, name, shape, dt):
        return self.nc.dram_tensor(name, list(shape), dt, kind="ExternalOutput").ap()

    def dram_int(self, name, shape, dt):
        return self.nc.dram_tensor(name, list(shape), dt).ap()

    def sb(self, name, shape, dt):
        return self.st.enter_context(self.nc.sbuf_tensor(name, list(shape), dt))

    def _declare(self):
        m = self.mode
        if m in ("A", "B", "F", "G"):
            self.x_in = self.dram_in("x", [T, D], F32)
        if m == "C":
            self.xt_in = self.dram_in("xt_in", [128, 16, T], F32)
        if m == "B":
            self.xt_out = self.dram_out("xt_out", [128, 16, T], F32)
        if m in ("C", "F"):
            self.y_out = self.dram_out("y", [T, D], F32)
        self.w_in = {}
        self.w_out = {}
        self.w1 = {}
        self.w2 = {}
        for l in self.layers:
            self.w_in[l] = self.dram_in("w_in%d" % l, [D, DIN], F32)
            full = not (m in ("A", "G") or (m == "B" and l == 1))
            if full:
                self.w_out[l] = self.dram_in("w_out%d" % l, [D, D], F32)
                self.w1[l] = self.dram_in("w1_%d" % l, [D, DFF], F32)
                self.w2[l] = self.dram_in("w2_%d" % l, [DFF, D], F32)
        self.gn_in = self.dram_in("gn", [128, 2, 2, 16], F32)
        self.cw_in = self.dram_in("cw", [128, 2, 6, 3], F32)
        self.qkg_in = self.dram_in("qkg", [128, 2, 2], F32)
        self.qkrow_in = self.dram_in("qkrow", [1, 2, 128], F32)
        self.sgwt_in = self.dram_in("sgwt", [128, 2, 8, 128], F32)
        self.sgb_in = self.dram_in("sgb", [2, 2, 4, 128], F32)
        self.cf_in = self.dram_in("cf", [128, 256], F32)
        self.cb_in = self.dram_in("cb", [128, 768], BF16)
        self.sel2_in = self.dram_in("sel2", [2, 128], F32)
        self.kx = {}
        self.vx = {}
        self.zx = {}
        self.kwin = {}
        self.vwin = {}
        self.zwin = {}
        for l in self.layers:
            export = (m == "A" and l == 0) or (m == "B" and l == 1)
            mk = self.dram_out if export else self.dram_int
            if m not in ("F", "G"):
                self.kx[l] = mk("kx%d" % l, [768, T], BF16)
                self.vx[l] = mk("vx%d" % l, [T, 768], BF16)
                self.zx[l] = mk("zx%d" % l, [768, 2], F32)
            post = (m == "B" and l == 0) or (m == "C" and l == 1)
            if post:
                self.kwin[l] = self.dram_in("kwin%d" % l, [3, 768, T], BF16)
                self.vwin[l] = self.dram_in("vwin%d" % l, [3 * T, 768], BF16)
                self.zwin[l] = self.dram_in("zwin%d" % l, [768, 2], F32)
        if m in ("F", "G"):
            for l in self.layers:
                self.kx[l] = self.dram_int("kx%d" % l, [768, T], BF16)
                self.vx[l] = self.dram_int("vx%d" % l, [T, 768], BF16)
                self.zx[l] = self.dram_int("zx%d" % l, [768, 2], F32)
                self.kwin[l] = [self.dram_int("kg%d_%d" % (l, h_), [6, 384, T], BF16) for h_ in range(2)]
                self.vwin[l] = [self.dram_int("vg%d_%d" % (l, h_), [6, 512, 768], BF16) for h_ in range(2)]
                self.zwin[l] = self.dram_int("zg%d" % l, [6, 768, 2], F32)
            self.kwl = {l: self.dram_int("kwl%d" % l, [3, 768, T], BF16) for l in self.layers}
            self.vwl = {l: self.dram_int("vwl%d" % l, [3 * T, 768], BF16) for l in self.layers}
            self.zwl = {l: self.dram_int("zwl%d" % l, [3, 768, 2], F32) for l in self.layers}
        self.XT = self.sb("XT", [128, 16, T], F32)
        self.HT = self.sb("HT", [128, 16, T], BF16)
        self.WS = [self.sb("WS%d" % i, [128, 8192], BF16) for i in range(3)]
        self.YB = [self.sb("YB%d" % i, [128, 6, T], BF16) for i in range(2)]
        self.S = self.sb("S", [128, 4, 1032], F32)
        self.BS = self.sb("BS", [128, 6, T], BF16)
        self.Z = self.S[:, 2, 0:T + 2]
        self.RS = self.S[:, 3, 0:T]
        self.SQ = self.BS[:, 4, :]
        self.KST = [self.BS[:, 0, :], self.BS[:, 1, :]]
        self.ET = [self.BS[:, 5, i * 256:(i + 1) * 256] for i in range(4)]
        self.VA = self.BS[:, 2:4, :].rearrange("p a t -> p (a t)").rearrange("p (n f) -> p n f", f=256)
        self.RT = [self.S[:, 0, 0:512], self.S[:, 1, 0:512]]
        self.GN = self.sb("GN", [128, 2, 2, 16], F32)
        self.CW = self.sb("CW", [128, 2, 6, 3], F32)
        self.QKG = self.sb("QKG", [128, 2, 2], F32)
        self.GQ8 = self.sb("GQ8", [128, 2], F32)
        self.QKROW = self.S[0:1, 3, 0:128]
        self.SGWT = self.S[:, 0, 0:T].rearrange("p (h i) -> p h i", h=8)
        self.WCT = self.sb("WCT", [128, 8, 128], BF16)
        self.SGB = self.S[0:2, 1, 0:512].rearrange("p (c i) -> p c i", c=4)
        self.CF = self.sb("CF", [128, 256], F32)
        self.CB = self.sb("CB", [128, 768], BF16)
        self.SEL2 = self.sb("SEL2", [2, 128], F32)
        self.ONESB = self.sb("ONESB", [128, 128], BF16)
        self.BLK = self.sb("BLK", [128, 128], BF16)
        self.ONEF = self.sb("ONEF", [1, 128], F32)
        self.EPSC = self.sb("EPSC", [128, 1], F32)
        self.NEGC = self.sb("NEGC", [128, 1], F32)
        self.TINY = self.sb("TINY", [1, 8], F32)
        self.ZL = self.sb("ZL", [128, 6, 2], F32)
        self.ZH = self.sb("ZH", [128, 6, 2], F32)
        self.ACC01 = self.sb("ACC01", [128, 6, 2], F32)
        self.B01 = self.sb("B01", [128, 6, 2], F32)
        self.FX = self.sb("FX", [128, 3, 6], F32)
        self.ps = [self.st.enter_context(self.nc.psum_tensor("ps%d" % i, [128, 512], F32)) for i in range(8)]

    def nb(self):
        b = self.bank
        self.bank = (self.bank + 1) % 8
        return b

    def ev_eng(self):
        self.evq += 1
        return "act" if self.evq % 2 else "dve"

    def mm(self, out, lhsT, rhs, start, stop, reads, writes, tp=None):
        if tp is None:
            fn = lambda h: h.matmul(out, lhsT=lhsT, rhs=rhs, start=start, stop=stop)
        else:
            fn = lambda h: h.matmul(out, lhsT=lhsT, rhs=rhs, start=start, stop=stop, tile_position=tp)
        return self.P.add("pe", fn, reads=reads, writes=writes)

    def tr(self, out, in_, reads, writes):
        idn = self.CF[:, 0:128]
        return self.P.add("pe", lambda h: h.transpose(out, in_, idn), reads=list(reads) + ["CF"], writes=writes)

    def act(self, out, in_, func, reads, writes, bias=None, scale=None):
        kw = {}
        if bias is not None:
            kw["bias"] = bias
        if scale is not None:
            kw["scale"] = scale
        return self.P.add("act", lambda h: h.activation(out=out, in_=in_, func=func, **kw), reads=reads, writes=writes)

    def copy(self, eng, out, in_, reads, writes):
        if eng == "act":
            return self.act(out, in_, AF.Copy, reads, writes)
        return self.P.add(eng, lambda h: h.tensor_copy(out=out, in_=in_), reads=reads, writes=writes)

    def tt(self, out, in0, in1, op, reads, writes, eng="dve"):
        return self.P.add(eng, lambda h: h.tensor_tensor(out=out, in0=in0, in1=in1, op=op), reads=reads, writes=writes)

    def stt(self, out, in0, scalar, in1, op0, op1, reads, writes, eng="dve"):
        return self.P.add(eng, lambda h: h.scalar_tensor_tensor(out=out, in0=in0, scalar=scalar, in1=in1, op0=op0, op1=op1),
                          reads=reads, writes=writes)

    def recip(self, out, in_, reads, writes):
        return self.P.add("dve", lambda h: h.reciprocal(out=out, in_=in_), reads=reads, writes=writes)

    def memset(self, ap, val, writes, eng="dve"):
        return self.P.add(eng, lambda h: h.memset(ap, val), writes=writes)

    def dma(self, eng, out, in_, reads, writes, key=None, grp=False):
        def fn(h):
            src = in_() if callable(in_) else in_
            try:
                return h.dma_start(out=out, in_=src)
            except Exception:
                print("DMA FAIL", writes, out, str(src)[:300])
                raise
        return self.P.add(eng, fn, reads=reads, writes=writes, dma=True, key=key, grp=grp)

    def wv(self, s, nk, ncols):
        return self.WS[s][:, 0:nk * ncols].rearrange("p (k n) -> p k n", k=nk)

    def plan_tiles(self):
        plan = []
        for l in self.layers:
            wi = self.w_in[l]

            def cols(c0, n, wi=wi):
                return wi[:, c0:c0 + n].rearrange("(kc k) n -> k kc n", k=128)

            def pre_tiles():
                for t in range(2):
                    plan.append([(16, 384, 0, 384, cols(384 * t, 384))])
                for t in range(2):
                    plan.append([(16, 384, 0, 384, cols(768 + 384 * t, 384))])
                for c in range(6):
                    plan.append([(16, 384, 0, 384, cols(1536 + 384 * c, 384))])

            def post_tiles(l=l):
                wo, w1, w2 = self.w_out[l], self.w1[l], self.w2[l]

                def rows(w, r0, nr):
                    return w[r0:r0 + nr * 128, :].rearrange("(rc r) n -> r rc n", r=128)

                for t in range(2):
                    plan.append([(16, 512, 0, 512, cols(3840 + 512 * t, 512))])
                plan.append([(4, 2048, 0, 2048, rows(wo, 0, 4))])
                plan.append([(3, 2048, 0, 2048, rows(wo, 512, 3))])
                plan.append([(3, 2048, 0, 2048, rows(wo, 896, 3))])
                for pr in range(2):
                    plan.append([(16, 384, 0, 384, cols(4864 + 384 * pr, 384))])
                plan.append([(3, 2048, 0, 2048, rows(wo, 1280, 3))])
                plan.append([(3, 2048, 0, 2048, rows(wo, 1664, 3))])
                seq = [("1", 0)]
                for g in range(16):
                    if g + 1 < 16:
                        seq.append(("1", g + 1))
                    seq.append(("2", g))
                for kind, g in seq:
                    if kind == "1":
                        plan.append([(16, 512, 0, 512, w1[:, 512 * g:512 * g + 512].rearrange("(kc k) n -> k kc n", k=128))])
                    else:
                        plan.append([(4, 2048, 0, 2048, rows(w2, 512 * g, 4))])

            m = self.mode
            if m in ("A", "G"):
                pre_tiles()
            elif m == "B":
                if l == 0:
                    pre_tiles(); post_tiles()
                else:
                    pre_tiles()
            elif m == "C":
                pre_tiles(); post_tiles()
            else:
                pre_tiles(); post_tiles()
        self.wplan = plan

    def issue_tile(self, i):
        s = i % 3
        for (nk, ncols, c0, n, src) in self.wplan[i]:
            dst = self.wv(s, nk, ncols)[:, :, c0:c0 + n]
            self.dma("pool", dst, src, reads=[], writes=[("WS", s)], grp=True)

    def get_tile(self):
        i = self.wcnt
        self.wcnt += 1
        while self.wissued < min(len(self.wplan), i + 3):
            self.issue_tile(self.wissued)
            self.wissued += 1
        return i % 3

    def fm(self, wview, ci, nk, rhs, rhs_res, wres):
        b = [self.nb(), self.nb()]
        for k in range(nk):
            for half in range(2):
                self.mm(self.ps[b[half]][:, :], wview[:, k, ci * 128:(ci + 1) * 128], rhs(k, half),
                        k == 0, k == nk - 1, reads=[wres] + rhs_res(k), writes=[("ps", b[half])])
        return b

    def ht_rhs(self, k, half):
        return self.HT[:, k, half * 512:(half + 1) * 512]

    def ht_res(self, k):
        return [("HT", k)]

    def consts(self):
        self.dma("sp", self.GN[:], self.gn_in, [], ["GN"])
        self.dma("sp", self.CW[:], self.cw_in, [], ["CW"])
        self.dma("sp", self.QKG[:], self.qkg_in, [], ["QKG"])
        self.dma("sp", self.CF[:], self.cf_in, [], ["CF"])
        self.dma("sp", self.CB[:], self.cb_in, [], ["CB"])
        self.dma("sp", self.SEL2[:], self.sel2_in, [], ["SEL2"])
        self.memset(self.ONESB[:], 1.0, ["ONESB"])
        self.memset(self.BLK[:], 0.0, ["BLK"])
        self.memset(self.BLK[0:64, 0:64], 1.0, ["BLK"])
        self.memset(self.BLK[64:128, 64:128], 1.0, ["BLK"])
        self.memset(self.ONEF[:], 1.0, ["ONEF"])
        self.memset(self.EPSC[:], EPS, ["EPSC"])
        self.P.add("dve", lambda h: h.tensor_scalar(out=self.GQ8[:], in0=self.QKG[:, :, 0], scalar1=0.125, scalar2=0.0,
                                                    op0=ALU.mult, op1=ALU.add), reads=["QKG"], writes=["GQ8"])

    def stage(self, sl):
        return self.S[:, 2 * sl:2 * sl + 2, 0:T], [("S", 2 * sl), ("S", 2 * sl + 1)]

    def load_x(self):
        if self.mode == "C":
            for k in range(16):
                self.dma("sp", self.XT[:, k, :], self.xt_in[:, k, :], [], [("XT", k)])
            return
        for tb in range(8):
            sl = tb % 2
            sv, sres = self.stage(sl)
            self.dma("sp", sv, self.x_in[tb * 128:(tb + 1) * 128, :].rearrange("p (a t) -> p a t", a=2), [], sres, key=("Sst", sl))
            for k4 in range(4):
                b = self.nb()
                for j in range(4):
                    k = 4 * k4 + j
                    self.tr(self.ps[b][:, j * 128:(j + 1) * 128], sv[:, k // 8, (k % 8) * 128:(k % 8 + 1) * 128], sres, [("ps", b)])
                self.copy(self.ev_eng(), self.XT[:, 4 * k4:4 * k4 + 4, tb * 128:(tb + 1) * 128],
                          self.ps[b][:, :].rearrange("p (j t) -> p j t", j=4),
                          [("ps", b)], [("XT", 4 * k4 + j) for j in range(4)])

    def store_xt(self):
        for k in range(16):
            o = self.dma("sp", self.xt_out[:, k, :], self.XT[:, k, :], [("XT", k)], [("xt_out", k)])
            self.finals.append(o)

    def store_y(self):
        for tb in range(8):
            sl = tb % 2
            sv, sres = self.stage(sl)
            for k4 in range(4):
                b = self.nb()
                for j in range(4):
                    k = 4 * k4 + j
                    self.tr(self.ps[b][:, j * 128:(j + 1) * 128], self.XT[:, k, tb * 128:(tb + 1) * 128], [("XT", k)], [("ps", b)])
                self.copy(self.ev_eng(), sv[:, k4 // 2, (k4 % 2) * 512:(k4 % 2 + 1) * 512], self.ps[b][:, :], [("ps", b)], sres)
            o = self.dma("sp", self.y_out[tb * 128:(tb + 1) * 128, :].rearrange("p (a t) -> p a t", a=2), sv, sres, [("y", tb)],
                         key=("yst", sl))
            self.finals.append(o)

    def norm(self, l, a):
        for k in range(16):
            self.act(self.HT[:, k, :], self.XT[:, k, :], AF.Square, [("XT", k)], [("HT", k)])
        b = [self.nb(), self.nb()]
        for k in range(16):
            for half in range(2):
                self.mm(self.ps[b[half]][:, :], self.ONESB[:, :], self.HT[:, k, half * 512:(half + 1) * 512], k == 0, k == 15,
                        reads=["ONESB", ("HT", k)], writes=[("ps", b[half])])
        for half in range(2):
            sl = slice(half * 512, (half + 1) * 512)
            self.act(self.RS[:, sl], self.ps[b[half]][:, :], AF.Sqrt, [("ps", b[half]), "EPSC"], [("S", 3)],
                     bias=self.EPSC[:, 0:1], scale=1.0 / D)
            self.recip(self.RS[:, sl], self.RS[:, sl], [("S", 3)], [("S", 3)])
        for k in range(16):
            for half in range(2):
                sl = slice(half * 512, (half + 1) * 512)
                self.stt(self.HT[:, k, sl], self.XT[:, k, sl], self.GN[:, l, a, k:k + 1], self.RS[:, sl], ALU.mult, ALU.mult,
                         [("XT", k), "GN", ("S", 3)], [("HT", k)])

    def headnorm(self, b, gain, out, out_res):
        for half in range(2):
            self.act(self.SQ[:, half * 512:(half + 1) * 512], self.ps[b[half]][:, :], AF.Square, [("ps", b[half])], [("SQ", half)])
        b2 = [self.nb(), self.nb()]
        for half in range(2):
            self.mm(self.ps[b2[half]][:, :], self.BLK[:, :], self.SQ[:, half * 512:(half + 1) * 512], True, True,
                    reads=["BLK", ("SQ", half)], writes=[("ps", b2[half])])
        for half in range(2):
            sl = slice(half * 512, (half + 1) * 512)
            self.act(self.RS[:, sl], self.ps[b2[half]][:, :], AF.Sqrt, [("ps", b2[half]), "EPSC"], [("S", 3)],
                     bias=self.EPSC[:, 0:1], scale=1.0 / 64)
            self.recip(self.RS[:, sl], self.RS[:, sl], [("S", 3)], [("S", 3)])
            self.stt(out[:, sl], self.ps[b[half]][:, :], gain, self.RS[:, sl], ALU.mult, ALU.mult,
                     [("ps", b[half]), ("S", 3), "QKG", "GQ8"], out_res)

    def kv(self, l):
        for t in range(2):
            s = self.get_tile()
            wvw = self.wv(s, 16, 384)
            for j in range(3):
                c = 3 * t + j
                b = self.fm(wvw, j, 16, self.ht_rhs, self.ht_res, ("WS", s))
                kst = c % 2
                self.headnorm(b, self.QKG[:, l, 1:2], self.KST[kst], [("BS", kst)])
                self.dma("sp", self.kx[l][c * 128:(c + 1) * 128, :], self.KST[kst], [("BS", kst)], [("kx", l)], key=("kx", l, kst))
        for t in range(2):
            s = self.get_tile()
            if t == 0 and self.mode in ("F", "G"):
                self.exchange_kv(l, 0)
                self.exchange_kv(l, 1)
            wvw = self.wv(s, 16, 384)
            vres = [("YB", 0, c) for c in range(3)]
            vst = self.YB[0][:, 0:3, :].rearrange("p c t -> p (c t)").rearrange("p (n f) -> p n f", f=384)
            for tb in range(8):
                b = self.nb()
                for k in range(16):
                    self.mm(self.ps[b][:, 0:384], self.HT[:, k, tb * 128:(tb + 1) * 128], wvw[:, k, :], k == 0, k == 15,
                            reads=[("WS", s), ("HT", k)], writes=[("ps", b)])
                self.copy(self.ev_eng(), vst[:, tb, :], self.ps[b][:, 0:384], [("ps", b)], vres)
            self.dma("sp", self.vx[l][:, 384 * t:384 * t + 384].rearrange("(tb p) f -> p tb f", p=128), vst, vres, [("vx", l)],
                     key=("vx", l), grp=True)

    def bmix(self, l):
        ZR = ("S", 2)
        self.memset(self.Z[:, 0:2], 0.0, [ZR])
        for c in range(6):
            s = self.get_tile()
            if c == 1 and self.mode in ("F", "G"):
                self.exchange_kv(l, 2)
                self.exchange_kv(l, 3)
            wvw = self.wv(s, 16, 384)
            bb = self.fm(wvw, 0, 16, self.ht_rhs, self.ht_res, ("WS", s))
            bc = self.fm(wvw, 1, 16, self.ht_rhs, self.ht_res, ("WS", s))
            bx = self.fm(wvw, 2, 16, self.ht_rhs, self.ht_res, ("WS", s))
            for half in range(2):
                sl = slice(half * 512, (half + 1) * 512)
                zsl = slice(2 + half * 512, 2 + (half + 1) * 512)
                self.act(self.S[:, 0, sl], self.ps[bc[half]][:, :], AF.Copy, [("ps", bc[half])], [("S", 0)])
                self.tt(self.Z[:, zsl], self.S[:, 0, sl], self.ps[bx[half]][:, :], ALU.mult, [("S", 0), ("ps", bx[half])], [ZR])
            self.copy("dve", self.ZL[:, c, :], self.Z[:, T:T + 2], [ZR], ["ZL"])
            self.act(self.S[:, 1, 0:T], self.Z[:, 2:T + 2], AF.Copy, [ZR, "CW"], [("S", 1)], scale=self.CW[:, l, c, 2:3])
            self.stt(self.S[:, 1, 0:T], self.Z[:, 1:T + 1], self.CW[:, l, c, 1:2], self.S[:, 1, 0:T], ALU.mult, ALU.add,
                     [ZR, "CW", ("S", 1)], [("S", 1)])
            self.stt(self.S[:, 1, 0:T], self.Z[:, 0:T], self.CW[:, l, c, 0:1], self.S[:, 1, 0:T], ALU.mult, ALU.add,
                     [ZR, "CW", ("S", 1)], [("S", 1)])
            self.copy("dve", self.ACC01[:, c, :], self.S[:, 1, 0:2], [("S", 1)], ["ACC01"])
            self.copy("dve", self.B01[:, c, :], self.ps[bb[0]][:, 0:2], [("ps", bb[0])], ["B01"])
            for half in range(2):
                sl = slice(half * 512, (half + 1) * 512)
                self.tt(self.YB[1][:, c, sl], self.S[:, 1, sl], self.ps[bb[half]][:, :], ALU.mult, [("S", 1), ("ps", bb[half])],
                        [("YB", 1, c)])
        self.dma("sp", self.zx[l].rearrange("(c p) t -> p c t", p=128), self.ZL[:, :, :], ["ZL"], [("zx", l)])

    def fixup(self, l):
        self.load_zh(l)
        fx = self.FX
        zh, cw, a01, b01 = self.ZH, self.CW, self.ACC01, self.B01
        R = ["ZH", "CW", "ACC01", "B01", "FX"]
        self.tt(fx[:, 0, :], zh[:, :, 0], cw[:, l, :, 0], ALU.mult, R, ["FX"])
        self.tt(fx[:, 0, :], fx[:, 0, :], a01[:, :, 0], ALU.add, R, ["FX"])
        self.tt(fx[:, 1, :], zh[:, :, 1], cw[:, l, :, 1], ALU.mult, R, ["FX"])
        self.tt(fx[:, 0, :], fx[:, 0, :], fx[:, 1, :], ALU.add, R, ["FX"])
        self.tt(fx[:, 2, :], zh[:, :, 1], cw[:, l, :, 0], ALU.mult, R, ["FX"])
        self.tt(fx[:, 2, :], fx[:, 2, :], a01[:, :, 1], ALU.add, R, ["FX"])
        yres = [("YB", 1, c) for c in range(6)]
        self.tt(self.YB[1][:, :, 0], fx[:, 0, :], b01[:, :, 0], ALU.mult, R, yres)
        self.tt(self.YB[1][:, :, 1], fx[:, 2, :], b01[:, :, 1], ALU.mult, R, yres)

    def wout(self, l, yb, c0, nr):
        s = self.get_tile()
        wvw = self.wv(s, nr, 2048)
        for dch in range(16):
            b = self.fm(wvw, dch, nr, lambda k, half: self.YB[yb][:, c0 + k, half * 512:(half + 1) * 512],
                        lambda k: [("YB", yb, c0 + k)], ("WS", s))
            for half in range(2):
                sl = slice(half * 512, (half + 1) * 512)
                self.tt(self.XT[:, dch, sl], self.XT[:, dch, sl], self.ps[b[half]][:, :], ALU.add,
                        [("XT", dch), ("ps", b[half])], [("XT", dch)])

    def amix(self, l):
        self.dma("sp", self.SGWT, self.sgwt_in[:, l, :, :], [], [("S", 0)])
        self.dma("sp", self.SGB, self.sgb_in[:, l, :, :], [], [("S", 1)])
        for h_ in range(8):
            self.tt(self.WCT[:, h_, :], self.SGWT[:, h_, :], self.CF[:, 128:256], ALU.mult, [("S", 0), "CF"], ["WCT"])
        VAR = [("BS", 2), ("BS", 3)]
        for t in range(2):
            s = self.get_tile()
            wvw = self.wv(s, 16, 512)
            for tb in range(8):
                b = self.nb()
                for k in range(16):
                    self.mm(self.ps[b][:, 0:256], self.HT[:, k, tb * 128:(tb + 1) * 128], wvw[:, k, 256:512], k == 0, k == 15,
                            reads=[("WS", s), ("HT", k)], writes=[("ps", b)])
                self.copy(self.ev_eng(), self.VA[:, tb, :], self.ps[b][:, 0:256], [("ps", b)], VAR)
            for cc in range(2):
                c = 2 * t + cc
                bu = self.fm(wvw, cc, 16, self.ht_rhs, self.ht_res, ("WS", s))
                for half in range(2):
                    self.act(self.S[:, 2, half * 512:(half + 1) * 512], self.ps[bu[half]][:, :], AF.Copy, [("ps", bu[half])], [("S", 2)])
                bm = [self.nb(), self.nb()]
                for tb in range(8):
                    half, off = tb // 4, (tb % 4) * 128
                    for e in range(2):
                        self.mm(self.ps[bm[half]][64 * e:64 * e + 64, off:off + 128],
                                self.VA[:, tb, cc * 128 + 64 * e:cc * 128 + 64 * e + 64], self.WCT[:, 2 * c + e, :], True, False,
                                reads=VAR + ["WCT"], writes=[("ps", bm[half])], tp=(0, 64 * e))
                    self.mm(self.ps[bm[half]][:, off:off + 128], self.SEL2[:, :], self.SGB[:, c, :], False, True,
                            reads=["SEL2", ("S", 1)], writes=[("ps", bm[half])])
                for half in range(2):
                    sl = slice(half * 512, (half + 1) * 512)
                    self.tt(self.YB[0][:, c, sl], self.S[:, 2, sl], self.ps[bm[half]][:, :], ALU.mult, [("S", 2), ("ps", bm[half])],
                            [("YB", 0, c)])

    def load_zh(self, l):
        if self.mode == "F":
            self.dma("sp", self.ZH[:], self.zwl[l][1].rearrange("(c p) t -> p c t", p=128), [("zwl", l)], ["ZH"])
        else:
            self.dma("sp", self.ZH[:], self.zwin[l].rearrange("(c p) t -> p c t", p=128), [], ["ZH"])

    def load_kwin(self, l, g, kc_, buf):
        d, hp = PATS[g]
        rows = slice(kc_ * 128, (kc_ + 1) * 128)
        KW = self.KWb[buf]
        res = [("HT", 3 * buf + i) for i in range(3)]
        F = self.mode == "F"
        rd = [("kwl", l)] if F else []
        kw = self.kwl[l] if F else self.kwin[l]
        key = ("KW", buf)
        if hp == 2048:
            self.dma("sp", KW[:, 0:T], kw[0, rows, :], rd, res, key=key, grp=True)
            self.dma("sp", KW[:, T:2 * T], kw[1, rows, :], rd, res, key=key, grp=True)
        else:
            self.dma("sp", KW[:, 0:hp], kw[1, rows, T - hp:T], rd, res, key=key, grp=True)
        self.dma("sp", KW[:, hp:hp + T], kw[2, rows, :], rd, res, key=key, grp=True)

    def load_vwin(self, l, g, kc_, buf):
        d, hp = PATS[g]
        L = T // d
        nk = 128 + L
        F = self.mode == "F"
        rd = [("vwl", l)] if F else []
        vw = self.vwl[l] if F else self.vwin[l]
        VW = self.VWb[buf]
        res = [("HT", 6 + 4 * buf + i) for i in range(4)]
        jb = 0
        while jb * 128 < nk:
            nj = min(128, nk - jb * 128)
            w0 = (2048 - hp) + d * 128 * jb
            src = vw[w0:w0 + d * nj, kc_ * 128:(kc_ + 1) * 128].rearrange("(j r) f -> j r f", r=d)
            self.dma("sp", VW[0:nj, jb * d:(jb + 1) * d, :], src, rd, res, key=("VW", buf), grp=True)
            jb += 1

    def attn_prep(self, l):
        tn = self.TINY
        QR = ("S", 3)
        self.dma("sp", self.QKROW, self.qkrow_in[:, l, :], [], [QR])
        R = [QR, "TINY"]
        self.tt(self.QKROW, self.QKROW, self.QKROW, ALU.mult, R, [QR])
        self.P.add("dve", lambda h: h.tensor_reduce(out=tn[:, 0:2], in_=self.QKROW.rearrange("p (a d) -> p a d", a=2),
                                                    axis=AX.X, op=ALU.max), reads=R, writes=["TINY"])
        self.tt(tn[:, 2:3], tn[:, 0:1], tn[:, 1:2], ALU.mult, R, ["TINY"])
        self.act(tn[:, 3:4], tn[:, 2:3], AF.Sqrt, R, ["TINY"], scale=64.0)
        self.P.add("dve", lambda h: h.tensor_scalar(out=tn[:, 4:5], in0=tn[:, 3:4], scalar1=-1.0, scalar2=0.0, op0=ALU.mult, op1=ALU.add),
                   reads=R, writes=["TINY"])
        b = self.nb()
        self.mm(self.ps[b][:, 0:1], self.ONEF[:, :], tn[:, 4:5], True, True, reads=["ONEF", "TINY"], writes=[("ps", b)])
        self.copy("dve", self.NEGC[:, :], self.ps[b][:, 0:1], [("ps", b)], ["NEGC"])

    def attention(self, l):
        self.attn_prep(l)
        self.KWb = [self.HT[:, 3 * i:3 * i + 3, :].rearrange("p c t -> p (c t)") for i in range(2)]
        self.VWb = [self.HT[:, 6 + 4 * i:10 + 4 * i, :].rearrange("p a t -> p (a t)").rearrange("p (n f) -> p n f", f=128) for i in range(2)]
        for pr in range(2):
            s = self.get_tile()
            wvw = self.wv(s, 16, 384)
            for g in range(3):
                b = self.fm(wvw, g, 16, self.ht_rhs, self.ht_res, ("WS", s))
                self.headnorm(b, self.GQ8[:, l:l + 1], self.YB[1][:, 3 * pr + g, :], [("YB", 1, 3 * pr + g)])
        st = dict(etc=0, scq=0, accq=0)
        groups = [(pr, g) for pr in range(2) for g in range(3)]
        items = []

        def mk_load(gi):
            pr, g = groups[gi]
            def back():
                self.load_kwin(l, g, 2 * g + pr, gi % 2)
                self.load_vwin(l, g, 2 * g + pr, gi % 2)
            return dict(front=None, back=back)

        def mk_final(pr):
            def back():
                self.recip(self.S[:, 3, 0:T], self.S[:, 3, 0:T], [("S", 3)], [("S", 3)])
                for g in range(3):
                    c = 2 * g + pr
                    self.tt(self.YB[0][:, c, :], self.S[:, g, 0:T], self.S[:, 3, 0:T], ALU.mult, [("S", g), ("S", 3)], [("YB", 0, c)])
            return dict(front=None, back=back)

        def mk_tile(gi, r, qb0, qb1, bn, bd, e, jb, last):
            pr, g = groups[gi]
            d, hp = PATS[g]
            L = T // d
            qn = min(L, 128)
            nqb = max(1, L // 128)
            qps = nqb // (2 if nqb == 8 else 1)
            buf = gi % 2
            KW, VW = self.KWb[buf], self.VWb[buf]
            KWR = [("HT", 3 * buf + i) for i in range(3)]
            VWR = [("HT", 6 + 4 * buf + i) for i in range(4)]
            qi = 3 * pr + g
            tl = {}

            def front():
                rows = slice(64 * e, 64 * e + 64)
                nj = min(128, 128 + L - jb * 128)
                parts = []
                if qb0 <= jb - 1 <= qb1:
                    parts.append((jb - 1, "hi"))
                if qb0 <= jb <= qb1:
                    parts.append((jb, "lo"))
                nq = qn * len(parts)
                q0 = parts[0][0] * qn
                kcol = slice(d * 128 * jb + r, d * (128 * jb + nj - 1) + r + 1, d)
                qcol = slice(d * q0 + r, d * (q0 + nq - 1) + r + 1, d)
                bs = 4 + st["scq"] % 4
                st["scq"] += 1
                self.mm(self.ps[bs][0:nj, 0:nq], KW[rows, kcol], self.YB[1][rows, qi, qcol], True, True,
                        reads=KWR + [("YB", 1, qi)], writes=[("ps", bs)])
                et = self.ET[st["etc"] % 4]
                eres = [("ET", st["etc"] % 4)]
                st["etc"] += 1
                self.act(et[0:nj, 0:nq], self.ps[bs][0:nj, 0:nq], AF.Exp, [("ps", bs), "NEGC"], eres,
                         bias=self.NEGC[0:nj, 0:1], scale=1.0)
                if jb == 0:
                    mk = self.CB[0:nj, 256 + 128 * g:256 + 128 * g + nq]
                elif len(parts) == 2:
                    mk = self.CB[0:nj, 0:256]
                elif parts[0][1] == "hi":
                    mk = self.CB[0:nj, 0:nq]
                else:
                    mk = self.CB[0:nj, 128:128 + nq]
                self.tt(et[0:nj, 0:nq], et[0:nj, 0:nq], mk, ALU.mult, eres + ["CB"], eres)
                tl.update(nj=nj, parts=parts, et=et, eres=eres, rows=rows)

            def back():
                nj, et, eres, rows = tl["nj"], tl["et"], tl["eres"], tl["rows"]
                for pi, (qb, kind) in enumerate(tl["parts"]):
                    oc = slice((qb - qb0) * qn, (qb - qb0 + 1) * qn)
                    first, lastp = (kind == "lo"), (kind == "hi")
                    rhs = et[0:nj, pi * qn:(pi + 1) * qn]
                    self.mm(self.ps[bn][rows, oc], VW[0:nj, jb * d + r, 64 * e:64 * e + 64], rhs, first, lastp,
                            reads=VWR + eres, writes=[("ps", bn)], tp=(0, 64 * e))
                    self.mm(self.ps[bd][rows, oc], self.ONESB[0:nj, 0:64], rhs, first, lastp,
                            reads=["ONESB"] + eres, writes=[("ps", bd)], tp=(0, 64 * e))
                if last:
                    ntok = qps * qn
                    i0 = qb0 * qn
                    tcol = slice(d * i0 + r, d * (i0 + ntok - 1) + r + 1, d)
                    self.act(self.S[:, g, tcol], self.ps[bn][:, 0:ntok], AF.Copy, [("ps", bn)], [("S", g)])
                    if g == 0:
                        self.copy("dve", self.S[:, 3, tcol], self.ps[bd][:, 0:ntok], [("ps", bd)], [("S", 3)])
                    else:
                        self.tt(self.S[:, 3, tcol], self.S[:, 3, tcol], self.ps[bd][:, 0:ntok], ALU.add, [("S", 3), ("ps", bd)],
                                [("S", 3)])
            return dict(front=front, back=back)

        mk_load(0)["back"]()
        mk_load(1)["back"]()
        for gi, (pr, g) in enumerate(groups):
            d, hp = PATS[g]
            L = T // d
            nqb = max(1, L // 128)
            nsup = 2 if nqb == 8 else 1
            qps = nqb // nsup
            if gi >= 1 and gi + 1 < len(groups):
                items.append(mk_load(gi + 1))
            for r in range(d):
                for sp_ in range(nsup):
                    qb0, qb1 = sp_ * qps, sp_ * qps + qps - 1
                    bn, bd = (0, 1) if st["accq"] % 2 == 0 else (2, 3)
                    st["accq"] += 1
                    gt = [(e, jb) for e in range(2) for jb in range(qb0, qb1 + 2)]
                    for ti, (e, jb) in enumerate(gt):
                        items.append(mk_tile(gi, r, qb0, qb1, bn, bd, e, jb, ti == len(gt) - 1))
            if g == 2:
                items.append(mk_final(pr))
        LA = 3
        for i in range(len(items) + LA):
            if i < len(items) and items[i]["front"] is not None:
                items[i]["front"]()
            if i >= LA:
                items[i - LA]["back"]()

    def mlp(self, l):
        self.norm(l, 1)

        def w1stage(g):
            s = self.get_tile()
            wvw = self.wv(s, 16, 512)
            hb = g % 2
            for f in range(4):
                b = self.fm(wvw, f, 16, self.ht_rhs, self.ht_res, ("WS", s))
                for half in range(2):
                    rt = self.RT[half]
                    self.act(rt, self.ps[b[half]][:, :], AF.Relu, [("ps", b[half])], [("S", half)])
                    self.act(self.YB[hb][:, f, half * 512:(half + 1) * 512], rt, AF.Square, [("S", half)], [("YB", hb, f)])

        def w2stage(g):
            self.wout(l, g % 2, 0, 4)

        w1stage(0)
        for g in range(16):
            if g + 1 < 16:
                w1stage(g + 1)
            w2stage(g)

    def zero_pads(self):
        zres = [("YB", 0, c) for c in range(6)]
        self.memset(self.YB[0][:, :, :], 0.0, zres)
        self.memset(self.FX[:, :, :], 0.0, ["FX"])
        zsrc = self.YB[0][:, :, :].rearrange("p c t -> p (c t)")
        for l in self.layers:
            for h_ in range(2):
                self.dma("sp", self.kwin[l][h_][0:2].rearrange("b (p a) t -> p b (a t)", p=128),
                         zsrc.rearrange("p (b x) -> p b x", b=2), zres, [("kgpad", l, h_)])
                self.dma("sp", self.vwin[l][h_][0:2].rearrange("b (p a) f -> p b (a f)", p=128),
                         zsrc.rearrange("p (b x) -> p b x", b=2), zres, [("vgpad", l, h_)])
            for bi in range(2):
                self.dma("sp", self.zwin[l][bi].rearrange("(p a) t -> p (a t)", p=128),
                         self.FX[:, :, :].rearrange("p a b -> p (a b)")[:, 0:12], ["FX"], [("zgpad", l, bi)])

        def setrv(h):
            r = h.partition_id() % 4
            self.rvb = [h.snap(r, min_val=0, max_val=3)]
            return None
        self.P.add("sp", setrv, reads=[], writes=["rv"], nosync=True)

    def _cc(self, src, dst, rres, wres, key):
        groups = [[0, 1, 2, 3], [4, 5, 6, 7]]
        self.P.add("pool", lambda h: h.collective_compute("AllGather", ALU.bypass, replica_groups=groups, ins=[src], outs=[dst]),
                   reads=rres, writes=wres, dma=True, key=key, inc=self.cc_inc)

    def exchange_kv(self, l, i):
        h_ = i % 2
        if i < 2:
            self._cc(self.kx[l][384 * h_:384 * h_ + 384, :], self.kwin[l][h_][2:6].rearrange("b f t -> (b f) t"),
                     [("kx", l), ("kgpad", l, h_)], [("kg", l, h_)], ("cck", l, h_))
        else:
            self._cc(self.vx[l][512 * h_:512 * h_ + 512, :], self.vwin[l][h_][2:6].rearrange("b t f -> (b t) f"),
                     [("vx", l), ("vgpad", l, h_)], [("vg", l, h_)], ("ccv", l, h_))

    def exchange_z(self, l):
        self._cc(self.zx[l], self.zwin[l][2:6].rearrange("b f t -> (b f) t"), [("zx", l), ("zgpad", l, 0), ("zgpad", l, 1)],
                 [("zg", l)], ("ccz", l))

    def localize(self, l):
        v3 = self.vwl[l].rearrange("(b t) f -> b t f", b=3)
        for h_ in range(2):
            self.dma("sp", self.kwl[l][:, 384 * h_:384 * h_ + 384, :],
                     (lambda h_=h_: self.kwin[l][h_][bass.ds(self.rvb[0], 3), :, :]), [("kg", l, h_), "rv"], [("kwl", l)], key=("kwl", l), grp=True)
            self.dma("sp", v3[:, 512 * h_:512 * h_ + 512, :],
                     (lambda h_=h_: self.vwin[l][h_][bass.ds(self.rvb[0], 3), :, :]), [("vg", l, h_), "rv"], [("vwl", l)], key=("vwl", l), grp=True)
        self.dma("sp", self.zwl[l].rearrange("b f t -> (b f) t"),
                 lambda: self.zwin[l][bass.ds(self.rvb[0], 3), :, :].rearrange("b f t -> (b f) t"), [("zg", l), "rv"], [("zwl", l)])

    def pre(self, l, do_kv=True):
        self.norm(l, 0)
        if do_kv:
            self.kv(l)
        else:
            for _ in range(4):
                self.get_tile()
        self.bmix(l)
        if self.mode in ("F", "G"):
            self.exchange_z(l)

    def post(self, l):
        self.amix(l)
        self.wout(l, 0, 0, 4)
        if self.mode == "F":
            self.localize(l)
        self.fixup(l)
        self.wout(l, 1, 0, 3)
        self.wout(l, 1, 3, 3)
        self.attention(l)
        self.wout(l, 0, 0, 3)
        self.wout(l, 0, 3, 3)
        self.mlp(l)

    def build(self, cc_inc=1):
        m = self.mode
        self.cc_inc = cc_inc
        self.plan_tiles()
        self.consts()
        self.load_x()
        if m == "F":
            self.zero_pads()
        if m == "A":
            self.pre(0)
            for key in (("kx", 0), ("vx", 0), ("zx", 0)):
                self.finals.append(self.P.last_w[key])
        elif m == "B":
            self.pre(0)
            self.post(0)
            self.pre(1)
            for key in (("kx", 1), ("vx", 1), ("zx", 1)):
                self.finals.append(self.P.last_w[key])
            self.store_xt()
        elif m == "C":
            self.pre(1)
            self.post(1)
            self.store_y()
        elif m == "G":
            self.zero_pads()
            self.pre(0)
            self.localize(0)
            self.gk = self.dram_out("gk", [3, 768, T], BF16)
            self.gv = self.dram_out("gv", [3 * T, 768], BF16)
            self.gz = self.dram_out("gz", [3, 768, 2], F32)
            self.finals.append(self.dma("sp", self.gk.rearrange("b f t -> (b f) t"), self.kwl[0].rearrange("b f t -> (b f) t"), [("kwl", 0)], ["gk"]))
            self.finals.append(self.dma("sp", self.gv, self.vwl[0], [("vwl", 0)], ["gv"]))
            self.finals.append(self.dma("sp", self.gz.rearrange("b f t -> (b f) t"), self.zwl[0].rearrange("b f t -> (b f) t"), [("zwl", 0)], ["gz"]))
        else:
            for l in (0, 1):
                self.pre(l)
                self.post(l)
            self.store_y()
        with self.nc.allow_non_contiguous_dma(reason="tiny halo / param layouts"):
            fin = list(self.finals)
            for e in ENGS:
                if self.P.q[e]:
                    fin.append(self.P.q[e][-1])
            self.P.emit(final_waits=fin)
        self.st.close()
        return self.nc


def _consts(rank):
    jj = np.arange(128)[:, None]
    ii = np.arange(128)[None, :]
    m_hi = (jj <= ii).astype(np.float32)
    m_lo = (jj >= ii).astype(np.float32)
    cf = np.concatenate([np.eye(128, dtype=np.float32), m_hi], axis=1)
    hm = []
    for g in range(3):
        if g < 2:
            valid = np.full((128, 1), 1.0 if rank >= 1 else 0.0, np.float32)
        else:
            valid = np.zeros((128, 1), np.float32)
            if rank >= 2:
                valid[:] = 1.0
            elif rank == 1:
                valid[64:] = 1.0
        hm.append(m_lo * valid)
    cb = np.concatenate([m_hi, m_lo] + hm + [np.eye(128, dtype=np.float32)], axis=1).astype(NPBF)
    sel2 = np.zeros((2, 128), np.float32)
    sel2[0, :64] = 1.0
    sel2[1, 64:] = 1.0
    return cf, cb, sel2


def _params(inp):
    gn = np.stack([inp["attn_norm"], inp["mlp_norm"]], axis=1)
    gn = np.ascontiguousarray(gn.reshape(2, 2, 16, 128).transpose(3, 0, 1, 2))
    cw = np.ascontiguousarray(inp["conv_w"].reshape(2, 3, 6, 128).transpose(3, 0, 2, 1))
    qk = np.stack([inp["q_norm"], inp["k_norm"]], axis=2)
    qkg = np.ascontiguousarray(np.concatenate([qk, qk], axis=1).transpose(1, 0, 2))
    qkrow = np.ascontiguousarray(np.concatenate([inp["q_norm"], inp["k_norm"]], axis=1)[None])
    sgwt = np.ascontiguousarray(inp["sgu_w"].transpose(3, 0, 1, 2))
    sgb = np.ascontiguousarray(inp["sgu_b"].reshape(2, 4, 2, 128).transpose(2, 0, 1, 3))
    return dict(gn=gn, cw=cw, qkg=qkg, qkrow=qkrow, sgwt=sgwt, sgb=sgb)


def _win_perm():
    p = list(range(4096, 4864)) + list(range(4864, 5632))
    for c in range(6):
        for base in (1024, 1792, 2560):
            p += list(range(base + 128 * c, base + 128 * c + 128))
    for t in range(2):
        p += list(range(256 * t, 256 * t + 256)) + list(range(512 + 256 * t, 512 + 256 * t + 256))
    for pr in range(2):
        for g in range(3):
            c0 = 3328 + 128 * (2 * g + pr)
            p += list(range(c0, c0 + 128))
    assert sorted(p) == list(range(DIN))
    return np.asarray(p)


_CACHE = {}


def _get_nc(mode, **kw):
    key = (mode, tuple(sorted(kw.items())))
    if key not in _CACHE:
        _CACHE[key] = Builder(mode).build(**kw)
    return _CACHE[key]


def _windows(kxs, vxs, zxs):
    outs = []
    for c in range(8):
        r = c % 4
        kw = np.zeros((3, 768, T), NPBF)
        vw = np.zeros((3 * T, 768), NPBF)
        zw = np.zeros((768, 2), np.float32)
        for bi in range(3):
            src = r - 2 + bi
            if src >= 0:
                kw[bi] = kxs[c - r + src]
                vw[bi * T:(bi + 1) * T] = vxs[c - r + src]
        if r >= 1:
            zw = zxs[c - 1]
        outs.append((kw, vw, zw))
    return outs


FUSED = True


def kernel(**inp):
    inp = {k: np.asarray(v) for k, v in inp.items()}
    x = np.ascontiguousarray(inp["x"].reshape(8, T, D))
    prm = _params(inp)
    base = []
    for c in range(8):
        cf, cb, sel2 = _consts(c % 4)
        d = dict(prm)
        d.update(cf=cf, cb=cb, sel2=sel2)
        base.append(d)
    W = {}
    for l in range(2):
        W["w_in%d" % l] = np.ascontiguousarray(inp["w_in"][l][:, _win_perm()])
        W["w_out%d" % l] = np.ascontiguousarray(inp["w_out"][l])
        W["w1_%d" % l] = np.ascontiguousarray(inp["w_mlp_in"][l])
        W["w2_%d" % l] = np.ascontiguousarray(inp["w_mlp_out"][l])
    cores = list(range(8))
    if FUSED:
        ncF = _get_nc("F")
        mapsF = [dict(base[c], x=x[c], **W) for c in cores]
        rF = run_bass_kernel_spmd(ncF, mapsF, core_ids=cores).results
        y = np.stack([rF[c]["y"] for c in cores], axis=0).reshape(2, 4 * T, D)
        return y.astype(np.float32)
    return _kernel_unfused(x, base, W)


def _kernel_unfused(x, base, W):
    cores = list(range(8))
    ncA = _get_nc("A")
    mapsA = [dict(base[c], x=x[c], w_in0=W["w_in0"]) for c in cores]
    rA = run_bass_kernel_spmd(ncA, mapsA, core_ids=cores).results
    win0 = _windows([r["kx0"] for r in rA], [r["vx0"] for r in rA], [r["zx0"] for r in rA])
    ncB = _get_nc("B")
    mapsB = [dict(base[c], x=x[c], w_in0=W["w_in0"], w_out0=W["w_out0"], w1_0=W["w1_0"], w2_0=W["w2_0"], w_in1=W["w_in1"],
                  kwin0=win0[c][0], vwin0=win0[c][1], zwin0=win0[c][2]) for c in cores]
    rB = run_bass_kernel_spmd(ncB, mapsB, core_ids=cores).results
    win1 = _windows([r["kx1"] for r in rB], [r["vx1"] for r in rB], [r["zx1"] for r in rB])
    ncC = _get_nc("C")
    mapsC = [dict(base[c], xt_in=rB[c]["xt_out"], w_in1=W["w_in1"], w_out1=W["w_out1"], w1_1=W["w1_1"], w2_1=W["w2_1"],
                  kwin1=win1[c][0], vwin1=win1[c][1], zwin1=win1[c][2]) for c in cores]
    rC = run_bass_kernel_spmd(ncC, mapsC, core_ids=cores).results
    y = np.stack([rC[c]["y"] for c in cores], axis=0).reshape(2, 4 * T, D)
    return y.astype(np.float32)
```
